# Optimizing a Trainium2 kernel written in Bass

```python
import jax, jax.numpy as jnp
from jax import lax
import numpy as np

D_MODEL = 2048
BATCH = 16
SEQ = 2048
DEPTH = 1
DEC_BATCH = 4
DEC_SEQ = 4096
PAST_LEN = 128

N_HEADS = 8
QK_DIM = 64
V_DIM = 2 * QK_DIM
D_ATTN = N_HEADS * V_DIM
D_Q = N_HEADS * 2 * QK_DIM
D_K = N_HEADS * 2 * QK_DIM
Q_BLOCK = 128
POOL_WINDOWS = (2, 4, 8, 16)
N_POOL_GROUPS = len(POOL_WINDOWS)
D_POOL = D_MODEL // 2
POOL_GROUP_DIM = D_POOL // N_POOL_GROUPS
D_IN = D_Q + D_K + D_ATTN + D_POOL + 2 * D_MODEL
SPLITS = (D_Q, D_Q + D_K, D_Q + D_K + D_ATTN, D_Q + D_K + D_ATTN + D_POOL,
          D_Q + D_K + D_ATTN + D_POOL + D_MODEL)
D_FF = 5632
EPS = 1e-6

kernel_name = "hybrid_diffattn_pool_macaron_encoder"


def rmsnorm(x, g):
    xf = x.astype(jnp.float32)
    y = xf * lax.rsqrt(jnp.mean(xf * xf, axis=-1, keepdims=True) + EPS)
    return y.astype(x.dtype) * g


def swiglu(x, w_gu, w_down):
    gate, up = jnp.split(x @ w_gu, 2, axis=-1)
    return (jax.nn.silu(gate) * up) @ w_down


def alibi_slopes():
    return jnp.asarray(2.0 ** (-8.0 * np.arange(1, N_HEADS + 1) / N_HEADS), dtype=jnp.float32)


def diff_attention(q, k, v, lam, subln_g, lambda_init):
    B, S = q.shape[0], q.shape[1]
    n_blk = S // Q_BLOCK
    scale = QK_DIM ** -0.5
    slopes = alibi_slopes()
    key_pos = jnp.arange(S)
    qb = jnp.moveaxis(q.reshape(B, n_blk, Q_BLOCK, N_HEADS, 2, QK_DIM), 1, 0)

    def one_block(args):
        qi, start = args
        s = jnp.einsum('bqhcd,bkhcd->bhcqk', qi, k).astype(jnp.float32) * scale
        q_pos = start + jnp.arange(Q_BLOCK)
        dist = jnp.abs(q_pos[:, None] - key_pos[None, :]).astype(jnp.float32)
        s = s - slopes[None, :, None, None, None] * dist
        p = jax.nn.softmax(s, axis=-1)
        w = p[:, :, 0] - lam * p[:, :, 1]
        return jnp.einsum('bhqk,bkhd->bqhd', w.astype(v.dtype), v)

    out = lax.map(one_block, (qb, jnp.arange(n_blk) * Q_BLOCK))
    out = jnp.moveaxis(out, 0, 1).reshape(B, S, N_HEADS, V_DIM)
    out = rmsnorm(out, subln_g) * (1.0 - lambda_init)
    return out.reshape(B, S, D_ATTN)


def pool_mixer(p, w_grp, scale):
    B, S = p.shape[0], p.shape[1]
    pg = p.reshape(B, S, N_POOL_GROUPS, POOL_GROUP_DIM).astype(jnp.float32)
    c = jnp.concatenate([jnp.zeros((B, 1, N_POOL_GROUPS, POOL_GROUP_DIM), jnp.float32),
                         jnp.cumsum(pg, axis=1)], axis=1)
    t = jnp.arange(S)
    means = []
    for g, w in enumerate(POOL_WINDOWS):
        lo = jnp.clip(t - w // 2, 0, S - 1)
        hi = jnp.clip(t + w // 2 - 1, 0, S - 1)
        cg = c[:, :, g]
        window_sum = cg[:, hi + 1] - cg[:, lo]
        means.append(window_sum / (hi - lo + 1).astype(jnp.float32)[None, :, None])
    pooled = jnp.stack(means, axis=2) - pg
    y = jnp.einsum('bsgc,gcd->bsgd', pooled.astype(p.dtype), w_grp)
    return y.reshape(B, S, D_POOL) * scale


def encoder_layer(x, l, ffn1_norm, ffn1_w_gu, ffn1_w_down, mix_norm, w_in,
                  lambda_q1, lambda_k1, lambda_q2, lambda_k2, attn_subln_g, w_attn_proj,
                  w_pool_grp, pool_scale, w_pool_proj, w_out,
                  ffn2_norm, ffn2_w_gu, ffn2_w_down):
    B, S = x.shape[0], x.shape[1]
    h = x + 0.5 * swiglu(rmsnorm(x, ffn1_norm[l]), ffn1_w_gu[l], ffn1_w_down[l])
    u = rmsnorm(h, mix_norm[l])
    z = u @ w_in[l]
    q, k, v, p_in, g_a, g_p = jnp.split(z, SPLITS, axis=-1)
    q = q.reshape(B, S, N_HEADS, 2, QK_DIM)
    k = k.reshape(B, S, N_HEADS, 2, QK_DIM)
    v = v.reshape(B, S, N_HEADS, V_DIM)
    lambda_init = 0.8 - 0.6 * float(np.exp(-0.3 * l))
    lam = (jnp.exp(jnp.sum(lambda_q1[l].astype(jnp.float32) * lambda_k1[l].astype(jnp.float32)))
           - jnp.exp(jnp.sum(lambda_q2[l].astype(jnp.float32) * lambda_k2[l].astype(jnp.float32)))
           + lambda_init)
    a = diff_attention(q, k, v, lam, attn_subln_g[l], lambda_init) @ w_attn_proj[l]
    p = pool_mixer(p_in, w_pool_grp[l], pool_scale[l]) @ w_pool_proj[l]
    merged = jax.nn.sigmoid(g_a) * a + jax.nn.sigmoid(g_p) * p
    h = h + merged @ w_out[l]
    h = h + 0.5 * swiglu(rmsnorm(h, ffn2_norm[l]), ffn2_w_gu[l], ffn2_w_down[l])
    return h


def setup_inputs(seed: int = 0) -> dict:
    key = jax.random.key(seed)
    ks = jax.random.split(key, 24)
    f32 = jnp.float32

    def nrm(k, shape, fan_in):
        return jax.random.normal(k, shape, f32) * (fan_in ** -0.5)

    def gain(k, shape):
        return jnp.ones(shape, f32) + 0.01 * jax.random.normal(k, shape, f32)

    L = DEPTH
    return {
        "x_prompt": jax.random.normal(ks[0], (BATCH, SEQ, D_MODEL), f32),
        "x_sample": jax.random.normal(ks[1], (DEC_BATCH, DEC_SEQ, D_MODEL), f32),
        "ffn1_norm": gain(ks[2], (L, D_MODEL)),
        "ffn1_w_gu": nrm(ks[3], (L, D_MODEL, 2 * D_FF), D_MODEL),
        "ffn1_w_down": nrm(ks[4], (L, D_FF, D_MODEL), D_FF),
        "mix_norm": gain(ks[5], (L, D_MODEL)),
        "w_in": nrm(ks[6], (L, D_MODEL, D_IN), D_MODEL),
        "lambda_q1": 0.1 * jax.random.normal(ks[7], (L, QK_DIM), f32),
        "lambda_k1": 0.1 * jax.random.normal(ks[8], (L, QK_DIM), f32),
        "lambda_q2": 0.1 * jax.random.normal(ks[9], (L, QK_DIM), f32),
        "lambda_k2": 0.1 * jax.random.normal(ks[10], (L, QK_DIM), f32),
        "attn_subln_g": gain(ks[11], (L, V_DIM)),
        "w_attn_proj": nrm(ks[12], (L, D_ATTN, D_MODEL), D_ATTN),
        "w_pool_grp": nrm(ks[13], (L, N_POOL_GROUPS, POOL_GROUP_DIM, POOL_GROUP_DIM), POOL_GROUP_DIM),
        "pool_scale": gain(ks[14], (L, D_POOL)),
        "w_pool_proj": nrm(ks[15], (L, D_POOL, D_MODEL), D_POOL),
        "w_out": nrm(ks[16], (L, D_MODEL, D_MODEL), D_MODEL),
        "ffn2_norm": gain(ks[17], (L, D_MODEL)),
        "ffn2_w_gu": nrm(ks[18], (L, D_MODEL, 2 * D_FF), D_MODEL),
        "ffn2_w_down": nrm(ks[19], (L, D_FF, D_MODEL), D_FF),
        "final_norm": gain(ks[20], (D_MODEL,)),
    }


def reference(x_prompt, x_sample, ffn1_norm, ffn1_w_gu, ffn1_w_down, mix_norm, w_in,
              lambda_q1, lambda_k1, lambda_q2, lambda_k2, attn_subln_g, w_attn_proj,
              w_pool_grp, pool_scale, w_pool_proj, w_out,
              ffn2_norm, ffn2_w_gu, ffn2_w_down, final_norm):
    def trunk(x):
        h = x
        for l in range(DEPTH):
            h = encoder_layer(h, l, ffn1_norm, ffn1_w_gu, ffn1_w_down, mix_norm, w_in,
                              lambda_q1, lambda_k1, lambda_q2, lambda_k2, attn_subln_g,
                              w_attn_proj, w_pool_grp, pool_scale, w_pool_proj, w_out,
                              ffn2_norm, ffn2_w_gu, ffn2_w_down)
        return rmsnorm(h, final_norm)

    y_prompt = trunk(x_prompt)
    y_sample = trunk(x_sample)
    return (y_prompt, y_sample)
```

```python
import numpy as np
import ml_dtypes
from contextlib import ExitStack

import concourse.bass as bass
import concourse.mybir as mybir
from concourse.bass_utils import run_bass_kernel_spmd

F32 = mybir.dt.float32
BF16 = mybir.dt.bfloat16
AF = mybir.ActivationFunctionType
ALU = mybir.AluOpType

D = 2048
DFF = 5632
NJ = DFF // 128
T = 512
NH = 8
EPS = 1e-6
LAMBDA_INIT = 0.8 - 0.6 * 1.0
SLOPES = [2.0 ** (-(h + 1)) for h in range(NH)]
QK_SCALE = 0.125
BIGPOS = float(2 ** 20)
WSLOT_ELEMS = 16 * 512
NWSLOT = 3


class Tok:
    __slots__ = ("sem", "val", "key")

    def __init__(self, sem, val, key):
        self.sem, self.val, self.key = sem, val, key


class Buf:
    __slots__ = ("name", "w", "r")

    def __init__(self, name):
        self.name = name
        self.w = None
        self.r = []


class Eng:
    def __init__(self, kb, e, name):
        self.kb, self.e, self.name = kb, e, name
        self.sem = kb.new_sem("e_" + name)
        self.key = "e_" + name
        self.cnt = 0
        self.seen = {}

    def wait(self, tok):
        if tok is None:
            return
        if self.name == "pe" and tok.key == self.key:
            return
        if self.seen.get(tok.key, 0) >= tok.val:
            return
        self.e.wait_ge(tok.sem, tok.val)
        self.seen[tok.key] = tok.val

    def signal(self, ins):
        self.cnt += 1
        ins.then_inc(self.sem, 1)
        t = Tok(self.sem, self.cnt, self.key)
        self.kb.latest[self.key] = t
        return t


class DmaSem:
    def __init__(self, kb, name):
        self.sem = kb.new_sem(name)
        self.key = name
        self.cnt = 0


class KB:
    def __init__(self, nt=12, ngroupA=8, dbg=None):
        self.nt = nt
        self.ntok = nt * T
        self.nA = min(ngroupA, nt)
        self.dbg = dbg
        self.nc = bass.Bass("TRN2", target_bir_lowering=False)
        self.stack = ExitStack()
        self.latest = {}
        self.nsem = 0
        nc = self.nc
        self.pe = Eng(self, nc.tensor, "pe")
        self.act = Eng(self, nc.scalar, "act")
        self.dve = Eng(self, nc.vector, "dve")
        self.pool = Eng(self, nc.gpsimd, "pool")
        self.sp = Eng(self, nc.sync, "sp")
        self.engs = [self.pe, self.act, self.dve, self.pool, self.sp]

    def new_sem(self, name):
        self.nsem += 1
        return self.stack.enter_context(self.nc.semaphore(name))

    def dram(self, name, shape, dt, kind):
        return self.nc.dram_tensor(name, list(shape), dt, kind=kind).ap()

    def sbuf(self, st, name, shape, dt):
        return st.enter_context(self.nc.sbuf_tensor(name, list(shape), dt))

    def deps(self, eng, reads, writes):
        for b in reads:
            eng.wait(b.w)
        for b in writes:
            eng.wait(b.w)
            for t in b.r:
                eng.wait(t)

    def commit(self, tok, reads, writes):
        for b in reads:
            b.r.append(tok)
            if len(b.r) > 24:
                d = {}
                for t in b.r:
                    if t.key not in d or d[t.key].val < t.val:
                        d[t.key] = t
                b.r = list(d.values())
        for b in writes:
            b.w = tok
            b.r = []

    def op(self, eng, fn, reads=(), writes=()):
        self.deps(eng, reads, writes)
        ins = fn(eng.e)
        tok = eng.signal(ins)
        self.commit(tok, reads, writes)
        return tok

    def mm_group(self, mms, reads=(), writes=(), transpose=False):
        eng = self.pe
        self.deps(eng, reads, writes)
        ins = None
        for m in mms:
            if transpose:
                ins = eng.e.transpose(m[0], m[1], m[2])
            else:
                ins = eng.e.matmul(m[0], m[1], m[2], start=m[3], stop=m[4])
        tok = eng.signal(ins)
        self.commit(tok, reads, writes)
        return tok

    def dma(self, eng, dsem, out, in_, reads=(), writes=(), n=1):
        self.deps(eng, reads, writes)
        eng.e.dma_start(out=out, in_=in_).then_inc(dsem.sem, 16)
        dsem.cnt += 16
        tok = Tok(dsem.sem, dsem.cnt, dsem.key)
        self.latest[dsem.key] = tok
        self.commit(tok, reads, writes)
        return tok

    def barrier(self, skip_prefix=None):
        toks = [t for k, t in self.latest.items() if not (skip_prefix and k.startswith(skip_prefix))]
        for e in self.engs:
            for t in toks:
                e.wait(t)


class ConstBatch:
    def __init__(self, kb, dsem):
        self.kb, self.dsem, self.bufs = kb, dsem, []

    def load(self, eng, out, in_):
        b = Buf("const")
        eng.e.dma_start(out=out, in_=in_).then_inc(self.dsem.sem, 16)
        self.dsem.cnt += 16
        self.bufs.append(b)
        return b

    def finish(self):
        tok = Tok(self.dsem.sem, self.dsem.cnt, self.dsem.key)
        self.kb.latest[self.dsem.key] = tok
        for b in self.bufs:
            b.w = tok
        self.bufs = []


class WeightSet:
    def __init__(self, kb, name, npanel, kc_list):
        self.name = name
        self.npanel = npanel
        self.kc = kc_list
        kcmax = max(kc_list)
        self.ap = kb.dram("wb_" + name, [npanel, 128, kcmax * 512], BF16, "Internal")
        self.cv = DmaSem(kb, "cv_" + name)
        self.tok = None
        self.ptok = {}


class WStream:
    def __init__(self, kb, st, seq, tag, nslot=NWSLOT):
        self.kb = kb
        self.seq = seq
        self.ns = nslot
        self.slots = [kb.sbuf(st, f"wslot{tag}{i}", [128, WSLOT_ELEMS], BF16) for i in range(nslot)]
        self.bufs = [Buf(f"wslot{i}") for i in range(nslot)]
        self.sems = [DmaSem(kb, f"wl{tag}{i}") for i in range(nslot)]
        self.nload = 0
        self.nuse = 0
        for _ in range(nslot):
            self._load()

    def _load(self):
        if self.nload >= len(self.seq):
            return
        ws, p = self.seq[self.nload]
        s = self.nload % self.ns
        kb = self.kb
        n = ws.kc[p] * 512
        kb.sp.wait(ws.ptok.get(p, ws.tok))
        kb.dma(kb.sp, self.sems[s], self.slots[s][:, 0:n], ws.ap[p, :, 0:n], writes=[self.bufs[s]])
        self.nload += 1

    def get(self, ws, p):
        assert self.seq[self.nuse] == (ws, p), (self.nuse, ws.name, p, self.seq[self.nuse][0].name, self.seq[self.nuse][1])
        s = self.nuse % self.ns
        kc = ws.kc[p]
        view = self.slots[s][:, 0:kc * 512].rearrange("p (k c) -> p k c", c=512)
        return view, self.bufs[s]

    def done(self):
        self.nuse += 1
        self._load()


def build_program(nt=12, nA=8, dbg=None):
    kb = KB(nt, nA, dbg)
    nc = kb.nc
    ntok = kb.ntok

    x_in = kb.dram("x", [ntok, D], F32, "ExternalInput")
    y_out = kb.dram("y", [ntok, D], F32, "ExternalOutput")
    w_ffn_gu = [kb.dram(f"ffn{i}_w_gu", [D, 2 * DFF], F32, "ExternalInput") for i in (1, 2)]
    w_ffn_dn = [kb.dram(f"ffn{i}_w_down", [DFF, D], F32, "ExternalInput") for i in (1, 2)]
    w_in = kb.dram("w_in", [D, 8192], F32, "ExternalInput")
    w_ap = kb.dram("w_attn_proj", [1024, D], F32, "ExternalInput")
    w_pp = kb.dram("w_pool_proj", [1024, D], F32, "ExternalInput")
    w_pg = kb.dram("w_pool_grp", [4, 256, 256], F32, "ExternalInput")
    w_o = kb.dram("w_out", [D, D], F32, "ExternalInput")
    norms = {n: kb.dram(n, [1, D], F32, "ExternalInput") for n in ("ffn1_norm", "mix_norm", "ffn2_norm", "final_norm")}
    lam_in = kb.dram("lam_vecs", [1, 256], F32, "ExternalInput")
    subln_in = kb.dram("attn_subln_g", [128, 1], F32, "ExternalInput")
    pscale_in = kb.dram("pool_scale", [128, 8], F32, "ExternalInput")
    ka_in = kb.dram("ka_tab", [3, 8, ntok], BF16, "ExternalInput")
    qa_in = kb.dram("qa_tab", [NH, 8, ntok], BF16, "ExternalInput")
    dbias_in = kb.dram("dbias_tab", [128, 896], F32, "ExternalInput")
    sident_in = kb.dram("sident_tab", [128, NH + 1, 128], BF16, "ExternalInput")
    icnt_in = kb.dram("icnt_tab", [4, ntok], F32, "ExternalInput")
    hmask_in = kb.dram("hmask_tab", [128, 2 * nt], F32, "ExternalInput")

    h_scr = kb.dram("h_scr", [ntok, D], F32, "Internal")
    h2_scr = kb.dram("h2_scr", [ntok, D], F32, "Internal")
    kt_scr = kb.dram("kt_scr", [NH, 128, ntok], BF16, "Internal")
    v_scr = kb.dram("v_scr", [NH, 128, ntok // 128, 128], BF16, "Internal")

    W = {}
    for i in (0, 1):
        W[f"gu{i}"] = WeightSet(kb, f"gu{i}", 22, [16] * 22)
        W[f"dn{i}"] = WeightSet(kb, f"dn{i}", 12, [16, 16, 12] * 4)
    W["in"] = WeightSet(kb, "in", 16, [16] * 16)
    W["ap"] = WeightSet(kb, "ap", 4, [8] * 4)
    W["pp"] = WeightSet(kb, "pp", 4, [8] * 4)
    W["o"] = WeightSet(kb, "o", 4, [16] * 4)
    pg_scr = kb.dram("wb_pg", [128, 8, 256], BF16, "Internal")
    pg_cv = DmaSem(kb, "cv_pg")

    def conv(ws, p, dst_c0, ncol, src, r0, kc, c0):
        dst = ws.ap[p, :, 0:kc * 512].rearrange("p (k c) -> p k c", c=512)[:, :, dst_c0:dst_c0 + ncol]
        s = src[r0:r0 + kc * 128, c0:c0 + ncol].rearrange("(k p) c -> p k c", p=128)
        kb.pool.e.dma_start(out=dst, in_=s).then_inc(ws.cv.sem, 16)
        ws.cv.cnt += 16

    def conv_finish(ws):
        ws.tok = Tok(ws.cv.sem, ws.cv.cnt, ws.cv.key)
        kb.latest[ws.cv.key] = ws.tok

    def conv_ffn(i, fine=False):
        g, d = W[f"gu{i}"], W[f"dn{i}"]
        base = g.cv
        for p in range(22):
            if fine and p % 3 == 0:
                g.cv = DmaSem(kb, f"cvf_gu{i}_{p}")
            conv(g, p, 0, 256, w_ffn_gu[i], 0, 16, 256 * p)
            conv(g, p, 256, 256, w_ffn_gu[i], 0, 16, DFF + 256 * p)
            if fine and (p % 3 == 2 or p == 21):
                conv_finish(g)
                for pp_ in range(p - p % 3, p + 1):
                    g.ptok[pp_] = g.tok
        if not fine:
            conv_finish(g)
        for dp in range(4):
            if fine:
                d.cv = DmaSem(kb, f"cvf_dn{i}_{dp}")
            for part, (j0, kc) in enumerate(((0, 16), (16, 16), (32, 12))):
                conv(d, dp * 3 + part, 0, 512, w_ffn_dn[i], j0 * 128, kc, dp * 512)
            if fine:
                conv_finish(d)
                for part in range(3):
                    d.ptok[dp * 3 + part] = d.tok
        if not fine:
            conv_finish(d)

    d0 = W["dn0"]
    for dp in range(4):
        d0.cv = DmaSem(kb, f"cvf_dn0_{dp}")
        for part, (j0, kc) in enumerate(((0, 16), (16, 16), (32, 12))):
            conv(d0, dp * 3 + part, 0, 512, w_ffn_dn[0], j0 * 128, kc, dp * 512)
        conv_finish(d0)
        for part in range(3):
            d0.ptok[dp * 3 + part] = d0.tok

    with ExitStack() as st0:
        NS = 3
        f32s = [kb.sbuf(st0, f"cvf{i}", [128, 16 * 512], F32) for i in range(NS)]
        b16s = [kb.sbuf(st0, f"cvb{i}", [128, 16 * 512], BF16) for i in range(NS)]
        fb = [Buf(f"cvf{i}") for i in range(NS)]
        bb3 = [[Buf(f"cvb{i}_{j}") for j in range(3)] for i in range(NS)]
        lds = [DmaSem(kb, f"cvld{i}") for i in range(NS)]
        sts = [DmaSem(kb, f"cvst{i}") for i in range(NS)]
        plist = []
        for p in range(22):
            plist.append((W["gu0"], p, 16, [(0, 256, w_ffn_gu[0], 0, 256 * p), (256, 256, w_ffn_gu[0], 0, DFF + 256 * p)]))
        for n_, (ws, p, kc, srcs) in enumerate(plist):
            i = n_ % NS
            nel = kc * 512
            fv = f32s[i][:, 0:nel].rearrange("p (k c) -> p k c", c=512)
            first = True
            for (dc0, ncol, src, r0_, c0_) in srcs:
                kb.dma(kb.sp, lds[i], fv[:, :, dc0:dc0 + ncol],
                       src[r0_:r0_ + kc * 128, c0_:c0_ + ncol].rearrange("(k p) c -> p k c", p=128),
                       writes=[fb[i]] if first else [])
                first = False
            fb[i].w = kb.latest[lds[i].key]
            cuts = [0, (nel * 3 // 8) // 512 * 512, nel]
            kb.op(kb.act, lambda e: e.activation(out=b16s[i][:, cuts[0]:cuts[1]], in_=f32s[i][:, cuts[0]:cuts[1]], func=AF.Copy),
                  reads=[fb[i]], writes=[bb3[i][0]])
            kb.op(kb.dve, lambda e: e.tensor_copy(b16s[i][:, cuts[1]:cuts[2]], f32s[i][:, cuts[1]:cuts[2]]),
                  reads=[fb[i]], writes=[bb3[i][1]])
            ws.ptok[p] = kb.dma(kb.sp, sts[i], ws.ap[p, :, 0:nel], b16s[i][:, 0:nel], reads=bb3[i][0:2])
        kb.barrier(skip_prefix=("cv_", "cvf_"))

    w_in_ws = W["in"]
    w_in_ws.cv = DmaSem(kb, "cvf_in_kv")
    for p in (2, 3, 4, 5):
        conv(w_in_ws, p, 0, 512, w_in, 0, 16, 512 * p)
    conv_finish(w_in_ws)
    for p in (2, 3, 4, 5):
        w_in_ws.ptok[p] = w_in_ws.tok
    w_in_ws.cv = DmaSem(kb, "cvf_in_rest")
    for p in (0, 1, 6, 7, 8, 9, 10, 11, 12, 13, 14, 15):
        conv(w_in_ws, p, 0, 512, w_in, 0, 16, 512 * p)
    conv_finish(w_in_ws)
    for p in range(4):
        conv(W["ap"], p, 0, 512, w_ap, 0, 8, 512 * p)
    conv_finish(W["ap"])
    for p in range(4):
        conv(W["pp"], p, 0, 512, w_pp, 0, 8, 512 * p)
    conv_finish(W["pp"])
    for g in range(4):
        kb.pool.e.dma_start(out=pg_scr[:, 2 * g:2 * g + 2, :],
                            in_=w_pg[g].rearrange("(k p) c -> p k c", p=128)).then_inc(pg_cv.sem, 16)
        pg_cv.cnt += 16
    pg_tok = Tok(pg_cv.sem, pg_cv.cnt, pg_cv.key)
    kb.latest[pg_cv.key] = pg_tok
    for p in range(4):
        conv(W["o"], p, 0, 512, w_o, 0, 16, 512 * p)
    conv_finish(W["o"])
    conv_ffn(1)

    banks = [kb.stack.enter_context(nc.psum_tensor(f"bank{i}", [128, 512], F32)) for i in range(8)]
    bbuf = [Buf(f"bank{i}") for i in range(8)]

    cst = kb.stack
    ident = kb.sbuf(cst, "ident", [128, NH + 1, 128], BF16)
    onesb = kb.sbuf(cst, "onesb", [128, 128], BF16)
    mhalf = kb.sbuf(cst, "mhalf", [128, 4], F32)
    c_sem = DmaSem(kb, "cload")
    cb = ConstBatch(kb, c_sem)
    c_buf = cb.load(kb.act, ident[:], sident_in[:, :, :])
    m_buf = Buf("memsets")
    kb.op(kb.dve, lambda e: e.memset(onesb[:], 1.0), writes=[m_buf])
    kb.op(kb.dve, lambda e: e.memset(mhalf[:], -0.5), writes=[m_buf])
    kb.op(kb.dve, lambda e: e.memset(mhalf[:, 1:2], EPS), writes=[m_buf])

    def load_gain(st, name):
        g = kb.sbuf(st, "g_" + name, [128, D], F32)
        gb = cb.load(kb.act, g[:], norms[name][0:1, :].broadcast_to([128, D]))
        return g, gb

    def rmsnorm_to_uT(xh, xhb, g, gb, uT, uTb, tmp):
        junk, junkb, ss, ssb, rstd, rstdb, xn, xnb = tmp

        def stage_a(ts):
            b = ts % 2
            kb.op(kb.act, lambda e: e.activation(out=junk[:], in_=xh[:, ts, :], func=AF.Square,
                                                 accum_out=ss[:, ts:ts + 1]),
                  reads=[xhb[ts]], writes=[junkb, ssb[ts]])
            kb.op(kb.act, lambda e: e.activation(out=ss[:, ts:ts + 1], in_=ss[:, ts:ts + 1], func=AF.Sqrt,
                                                 scale=1.0 / D, bias=mhalf[:, 1:2]),
                  reads=[ssb[ts], m_buf], writes=[ssb[ts]])
            kb.op(kb.dve, lambda e: e.reciprocal(out=rstd[:, ts:ts + 1], in_=ss[:, ts:ts + 1]),
                  reads=[ssb[ts]], writes=[rstdb[ts]])
            kb.op(kb.dve, lambda e: e.scalar_tensor_tensor(out=xn[b][:], in0=xh[:, ts, :], scalar=rstd[:, ts:ts + 1],
                                                           in1=g[:], op0=ALU.mult, op1=ALU.mult),
                  reads=[xhb[ts], rstdb[ts], gb], writes=[xnb[b]])

        def stage_b(ts):
            b = ts % 2
            for half in range(2):
                bk = 4 + 2 * b + half
                pv = banks[bk].bitcast(BF16)
                mms = []
                for k in range(8):
                    dc = half * 8 + k
                    mms.append((pv[:, k * 128:(k + 1) * 128], xn[b][:, dc * 128:(dc + 1) * 128], ident[:, 0, :]))
                kb.mm_group(mms, reads=[xnb[b], c_buf], writes=[bbuf[bk]], transpose=True)
                src = pv[:, :].rearrange("p (k t) -> p k t", t=128)
                dst = uT[:, half * 8:half * 8 + 8, ts * 128:(ts + 1) * 128]
                ub = uTb[ts * 2 + half]
                if half == 0:
                    kb.op(kb.act, lambda e: e.activation(out=dst, in_=src, func=AF.Copy), reads=[bbuf[bk]], writes=[ub])
                else:
                    kb.op(kb.dve, lambda e: e.tensor_copy(dst, src), reads=[bbuf[bk]], writes=[ub])

        stage_a(0)
        stage_a(1)
        stage_b(0)
        stage_a(2)
        stage_b(1)
        stage_a(3)
        stage_b(2)
        stage_b(3)

    def ffn(stream, wg, wd, uT, uTb, actT, actTb, xh, xhb, sg, sgb, after_dp=None):
        for p in range(22):
            wv, wb = stream.get(wg, p)
            for c in range(2):
                j = 2 * p + c
                pb = (j % 2) * 2
                bg, bu = pb, pb + 1
                kb.mm_group([(banks[bg][:], wv[:, k, c * 128:(c + 1) * 128], uT[:, k, :], k == 0, k == 15) for k in range(16)],
                            reads=[wb] + uTb, writes=[bbuf[bg]])
                kb.mm_group([(banks[bu][:], wv[:, k, 256 + c * 128:256 + (c + 1) * 128], uT[:, k, :], k == 0, k == 15) for k in range(16)],
                            reads=[wb] + uTb, writes=[bbuf[bu]])
                s = j % 2
                kb.op(kb.act, lambda e: e.activation(out=sg[s][:], in_=banks[bg][:], func=AF.Silu),
                      reads=[bbuf[bg]], writes=[sgb[s]])
                kb.op(kb.dve, lambda e: e.tensor_tensor(out=actT[:, j, :], in0=sg[s][:], in1=banks[bu][:], op=ALU.mult),
                      reads=[sgb[s], bbuf[bu]], writes=[actTb[j]])
            stream.done()
        for dp in range(4):
            for part, (j0, kc) in enumerate(((0, 16), (16, 16), (32, 12))):
                wv, wb = stream.get(wd, dp * 3 + part)
                for ts in range(4):
                    bk = 4 + ts
                    mms = [(banks[bk][:], actT[:, j0 + k, ts * 128:(ts + 1) * 128], wv[:, k, :],
                            part == 0 and k == 0, part == 2 and k == kc - 1) for k in range(kc)]
                    kb.mm_group(mms, reads=[wb] + [actTb[j0 + k] for k in range(kc)],
                                writes=[bbuf[bk]] if part == 0 else [], )
                    if part == 2:
                        pass
                stream.done()
            for ts in range(4):
                bk = 4 + ts
                bbuf[bk].w = kb.latest[kb.pe.key]
                dst = xh[:, ts, dp * 512:(dp + 1) * 512]
                kb.op(kb.dve, lambda e: e.scalar_tensor_tensor(out=dst, in0=banks[bk][:], scalar=0.5, in1=dst,
                                                               op0=ALU.mult, op1=ALU.add),
                      reads=[bbuf[bk]], writes=[xhb[ts]])
            if after_dp is not None:
                after_dp(dp)

    with ExitStack() as st:
        seq = []
        for t in range(nt):
            seq += [(W["gu0"], p) for p in range(22)] + [(W["dn0"], p) for p in range(12)]
            seq += [(W["in"], p) for p in (2, 3, 4, 5)]
        stream = WStream(kb, st, seq, "a", nslot=4)
        xh = kb.sbuf(st, "xh", [128, 4, D], F32)
        xhb = [Buf(f"xh{ts}") for ts in range(4)]
        xsem = [DmaSem(kb, f"p1x{ts}") for ts in range(4)]
        g1, g1b = load_gain(st, "ffn1_norm")
        g2, g2b = load_gain(st, "mix_norm")
        cb.finish()
        uT = kb.sbuf(st, "uT", [128, 16, T], BF16)
        uTb = [Buf(f"uT{i}") for i in range(8)]
        actT = kb.sbuf(st, "actT", [128, NJ, T], BF16)
        actTb = [Buf(f"actT{j}") for j in range(NJ)]
        sg = [kb.sbuf(st, f"sg{i}", [128, T], F32) for i in range(2)]
        sgb = [Buf(f"sg{i}") for i in range(2)]
        junk = kb.sbuf(st, "junk", [128, D], BF16)
        ss = kb.sbuf(st, "ss", [128, 4], F32)
        rstd = kb.sbuf(st, "rstd", [128, 4], F32)
        xn = [kb.sbuf(st, f"xn{i}", [128, D], BF16) for i in range(2)]
        tmp = (junk, Buf("junk"), ss, [Buf(f"ss{i}") for i in range(4)], rstd, [Buf(f"rstd{i}") for i in range(4)],
               xn, [Buf("xn0"), Buf("xn1")])
        kst = [kb.sbuf(st, f"kst{i}", [128, T], BF16) for i in range(2)]
        kstb = [Buf(f"kst{i}") for i in range(2)]
        ksem = [DmaSem(kb, f"kst{i}") for i in range(2)]
        vst = kb.sbuf(st, "vst", [128, NH, 4, 128], BF16)
        vstb = [Buf(f"vst{i}") for i in range(8)]
        vsem = DmaSem(kb, "vst")

        def load_x(t):
            for ts in range(4):
                kb.dma(kb.act, xsem[ts], xh[:, ts, :], x_in[t * T + ts * 128:t * T + (ts + 1) * 128, :], writes=[xhb[ts]])

        load_x(0)
        for t in range(nt):
            r0 = t * T
            rmsnorm_to_uT(xh, xhb, g1, g1b, uT, uTb, tmp)
            ffn(stream, W["gu0"], W["dn0"], uT, uTb, actT, actTb, xh, xhb, sg, sgb)
            for ts in range(4):
                kb.dma(kb.act, xsem[ts], h_scr[r0 + ts * 128:r0 + (ts + 1) * 128, :], xh[:, ts, :], reads=[xhb[ts]])
            rmsnorm_to_uT(xh, xhb, g2, g2b, uT, uTb, tmp)
            if t + 1 < nt:
                load_x(t + 1)
            for pi, p in enumerate((2, 3)):
                wv, wb = stream.get(W["in"], p)
                for c in range(4):
                    h = 4 * pi + c
                    bk = h % 4
                    kb.mm_group([(banks[bk][:], wv[:, k, c * 128:(c + 1) * 128], uT[:, k, :], k == 0, k == 15) for k in range(16)],
                                reads=[wb] + uTb, writes=[bbuf[bk]])
                    s = h % 2
                    if s == 0:
                        kb.op(kb.act, lambda e: e.activation(out=kst[s][:], in_=banks[bk][:], func=AF.Copy),
                              reads=[bbuf[bk]], writes=[kstb[s]])
                    else:
                        kb.op(kb.dve, lambda e: e.tensor_copy(kst[s][:], banks[bk][:]), reads=[bbuf[bk]], writes=[kstb[s]])
                    kb.dma(kb.act, ksem[s], kt_scr[h, :, r0:r0 + T], kst[s][:], reads=[kstb[s]])
                stream.done()
            for pi, p in enumerate((4, 5)):
                wv, wb = stream.get(W["in"], p)
                for ts in range(4):
                    bk = 4 + ts
                    kb.mm_group([(banks[bk][:], uT[:, k, ts * 128:(ts + 1) * 128], wv[:, k, :], k == 0, k == 15) for k in range(16)],
                                reads=[wb] + uTb, writes=[bbuf[bk]])
                    dst = vst[:, 4 * pi:4 * pi + 4, ts, :]
                    src = banks[bk][:, :].rearrange("p (h d) -> p h d", d=128)
                    if ts % 2 == 0:
                        kb.op(kb.act, lambda e: e.activation(out=dst, in_=src, func=AF.Copy), reads=[bbuf[bk]], writes=[vstb[pi * 4 + ts]])
                    else:
                        kb.op(kb.dve, lambda e: e.tensor_copy(dst, src), reads=[bbuf[bk]], writes=[vstb[pi * 4 + ts]])
                stream.done()
            kb.dma(kb.act, vsem, v_scr[:, :, 4 * t:4 * t + 4, :].rearrange("h p s d -> p h s d"), vst[:],
                   reads=vstb)
        kb.barrier()

    if dbg == "p1":
        dsem = DmaSem(kb, "dbg")
        kb.dma(kb.sp, dsem, y_out[:, :], h_scr[:, :])
        kb.sp.wait(kb.latest["dbg"])
        return kb

    nA = kb.nA
    with ExitStack() as st:
        seq = []
        for t in range(nt):
            seq += [(W["in"], p) for p in (6, 0, 7, 1)]
            for mg in range(4):
                seq += [(W["in"], 12 + mg), (W["pp"], mg), (W["in"], 8 + mg), (W["ap"], mg)]
            seq += [(W["o"], p) for p in range(4)]
        stream = WStream(kb, st, seq, "b")
        cb2 = ConstBatch(kb, c_sem)
        gm, gmb = (lambda g: (g, cb2.load(kb.act, g[:], norms["mix_norm"][0:1, :].broadcast_to([128, D]))))(
            kb.sbuf(st, "g_mix2", [128, D], F32))
        hs = [kb.sbuf(st, f"hs{i}", [128, D], F32) for i in range(2)]
        hsb = [Buf(f"hs{i}") for i in range(2)]
        hsem = [DmaSem(kb, f"hs{i}") for i in range(2)]
        xn = [kb.sbuf(st, f"xnb{i}", [128, D], BF16) for i in range(2)]
        xnb = [Buf("xnb0"), Buf("xnb1")]
        ss = kb.sbuf(st, "ss2", [128, 8], F32)
        ssb = [Buf(f"ss2{i}") for i in range(5)]
        rstd = kb.sbuf(st, "rstd2", [128, 8], F32)
        rstdb = [Buf(f"rstd2{i}") for i in range(5)]
        uT = kb.sbuf(st, "uT2", [128, 16, T], BF16)
        uTb = [Buf(f"uT2{i}") for i in range(8)]
        uTh = kb.sbuf(st, "uTh", [128, 16, 16], BF16)
        uThb = Buf("uTh")
        qT = kb.sbuf(st, "qT", [128, NH, T], BF16)
        qTb = [Buf(f"qT{h}") for h in range(NH)]
        qaugb = [Buf(f"qaug{h}") for h in range(NH)]
        xaugb = [Buf(f"xaug{h}") for h in range(NH)]
        qasem = DmaSem(kb, "qaug")
        nkA = nA * 4
        nkB = (nt - nA) * 4
        kb.op(kb.pool, lambda e: e.memset(qT[64:128, :, :], 0.0), writes=qTb)
        dbias = kb.sbuf(st, "dbias", [128, 896], F32)
        dbb = cb2.load(kb.act, dbias[:], dbias_in[:, :])
        HK = 8
        kx = [kb.sbuf(st, f"kx{i}", [128, HK * 128], BF16) for i in range(2)]
        ky = [kb.sbuf(st, f"ky{i}", [128, HK * 128], BF16) for i in range(2)]
        vbuf = [kb.sbuf(st, f"vbuf{i}", [128, HK, 128], BF16) for i in range(2)]
        kvb = [Buf("kv0"), Buf("kv1")]
        kvsem = [DmaSem(kb, "kv0"), DmaSem(kb, "kv1")]
        for i_ in range(2):
            kb.op(kb.pool, lambda e: e.memset(kx[i_][:], 0.0), writes=[kvb[i_]])
            kb.op(kb.pool, lambda e: e.memset(ky[i_][:], 0.0), writes=[kvb[i_]])
        rl0 = kb.sbuf(st, "rl0", [128, T], F32)
        rl0b = Buf("rl0")
        Et = [[kb.sbuf(st, f"E{i}{c}", [128, T], BF16) for c in range(2)] for i in range(2)]
        Eb = [[Buf(f"E{i}{c}") for c in range(2)] for i in range(2)]
        r0 = kb.sbuf(st, "r0", [128, T], F32)
        r1 = kb.sbuf(st, "r1", [128, T], F32)
        r0b, r1b = Buf("r0"), Buf("r1")
        sq = kb.sbuf(st, "sq", [128, T], BF16)
        sqb = Buf("sq")
        acc2 = [[kb.sbuf(st, f"acc{p}{c}", [128, T], F32) for c in range(2)] for p in range(2)]
        accb2 = [[Buf(f"acc{p}{c}") for c in range(2)] for p in range(2)]
        hblk_t = [r0, r1, rl0, acc2[0][0], acc2[0][1], acc2[1][0], acc2[1][1]]
        hblk_b = [r0b, r1b, rl0b, accb2[0][0], accb2[0][1], accb2[1][0], accb2[1][1]]
        hbsem = [DmaSem(kb, f"hblk{i}") for i in range(7)]
        ones32 = kb.sbuf(st, "ones32", [128, 128], F32)
        kb.op(kb.pool, lambda e: e.memset(ones32[:], 1.0), writes=[m_buf])
        aT = kb.sbuf(st, "aT", [128, NH, T], BF16)
        aTb = [Buf(f"aT{h}") for h in range(NH)]
        pin = kb.sbuf(st, "pin", [128, 2, T + 16], F32)
        TA = kb.sbuf(st, "TA", [128, 2, T + 16], F32)
        TB = kb.sbuf(st, "TB", [128, 2, T + 16], F32)
        pinb, TAb, TBb = Buf("pin"), Buf("TA"), Buf("TB")
        pooledT = [kb.sbuf(st, f"pooledT{i}", [128, 4 * T], BF16) for i in range(2)]
        pldb = [Buf("pld0"), Buf("pld1")]
        ppT = kb.sbuf(st, "ppT", [128, 8, T], BF16)
        ppTb = [Buf(f"ppT{i}") for i in range(8)]
        icnt = [kb.sbuf(st, f"icnt{i}", [128, T], F32) for i in range(2)]
        icb = [Buf("icnt0"), Buf("icnt1")]
        icsem = [DmaSem(kb, "ic0"), DmaSem(kb, "ic1")]
        pgw = kb.sbuf(st, "pgw", [128, 8, 256], BF16)
        kb.act.wait(pg_tok)
        pgb = cb2.load(kb.act, pgw[:], pg_scr[:, :, :])
        mg_t = kb.sbuf(st, "merged", [128, 16, T], BF16)
        mgb = [Buf(f"mg{i}") for i in range(16)]
        lamv = kb.sbuf(st, "lamv", [128, 256], F32)
        lamb_ = cb2.load(kb.act, lamv[:], lam_in[0:1, :].broadcast_to([128, 256]))
        gsub = kb.sbuf(st, "gsub", [128, 1], F32)
        gsubb = cb2.load(kb.act, gsub[:], subln_in[:, :])
        psc = kb.sbuf(st, "psc", [128, 8], F32)
        pscb = cb2.load(kb.act, psc[:], pscale_in[:, :])
        hmask = kb.sbuf(st, "hmask", [128, 2 * nt], F32)
        hmb = cb2.load(kb.act, hmask[:], hmask_in[:, :])
        cb2.finish()
        small = kb.sbuf(st, "small", [128, 8], F32)
        smb = Buf("small")
        ljunk = kb.sbuf(st, "ljunk", [128, 64], F32)

        kb.op(kb.dve, lambda e: e.scalar_tensor_tensor(out=ljunk[:], in0=lamv[:, 0:64], scalar=1.0, in1=lamv[:, 64:128],
                                                       op0=ALU.mult, op1=ALU.mult, accum_out=small[:, 0:1]),
              reads=[lamb_], writes=[smb])
        kb.op(kb.dve, lambda e: e.scalar_tensor_tensor(out=ljunk[:], in0=lamv[:, 128:192], scalar=1.0, in1=lamv[:, 192:256],
                                                       op0=ALU.mult, op1=ALU.mult, accum_out=small[:, 1:2]),
              reads=[lamb_, smb], writes=[smb])
        kb.op(kb.act, lambda e: e.activation(out=small[:, 2:4], in_=small[:, 0:2], func=AF.Exp), reads=[smb], writes=[smb])
        kb.op(kb.dve, lambda e: e.tensor_tensor(out=small[:, 4:5], in0=small[:, 3:4], in1=small[:, 2:3], op=ALU.subtract),
              reads=[smb], writes=[smb])
        kb.op(kb.dve, lambda e: e.tensor_scalar(out=small[:, 5:6], in0=small[:, 4:5], scalar1=-LAMBDA_INIT, scalar2=None,
                                                op0=ALU.add), reads=[smb], writes=[smb])
        kb.op(kb.dve, lambda e: e.tensor_scalar(out=small[:, 6:7], in0=gsub[:, 0:1], scalar1=1.0 - LAMBDA_INIT, scalar2=None,
                                                op0=ALU.mult), reads=[smb, gsubb], writes=[smb])
        neglam = small[:, 5:6]
        gsc = small[:, 6:7]

        def group_of(t):
            return (0, nkA, 0) if t < nA else (nA * 4, nkB, 1)

        def load_kv(h, kt0, n, slot, qt):
            k0g_, _, _ = group_of(qt)
            c0, c1 = kt0 * 128, (kt0 + n) * 128
            kb.dma(kb.sp, kvsem[slot], kx[slot][0:64, 0:n * 128], kt_scr[h, 0:64, c0:c1], writes=[kvb[slot]])
            kb.dma(kb.sp, kvsem[slot], ky[slot][64:128, 0:n * 128], kt_scr[h, 64:128, c0:c1], writes=[])
            kb.dma(kb.sp, kvsem[slot], vbuf[slot][:, 0:n, :], v_scr[h, :, kt0:kt0 + n, :], writes=[])
            qlo = qt * 4
            segs = []
            for (a_, b_, v_) in ((kt0, min(kt0 + n, qlo), 0), (max(kt0, qlo), min(kt0 + n, qlo + 4), 2),
                                 (max(kt0, qlo + 4), kt0 + n, 1)):
                if b_ > a_:
                    segs.append((a_, b_, v_))
            for (a_, b_, v_) in segs:
                kb.dma(kb.sp, kvsem[slot], kx[slot][64:72, (a_ - kt0) * 128:(b_ - kt0) * 128],
                       ka_in[v_, :, a_ * 128:b_ * 128], writes=[])
                kb.dma(kb.sp, kvsem[slot], ky[slot][32:40, (a_ - kt0) * 128:(b_ - kt0) * 128],
                       ka_in[v_, :, a_ * 128:b_ * 128], writes=[])
            kvb[slot].w = kb.latest[kvsem[slot].key]

        kvcount = [0]

        def kv_plan(t):
            k0, nk, _ = group_of(t)
            plan = []
            for h in range(NH):
                for c0 in range(0, nk, HK):
                    plan.append((h, k0 + c0, min(HK, nk - c0)))
            return plan

        for t in range(nt):
            rbase = t * T
            k0g, nkg, gid = group_of(t)
            plan = kv_plan(t)
            pl_i = 0
            slot_of = {}
            for _ in range(min(2, len(plan))):
                slot = kvcount[0] % 2
                load_kv(*plan[pl_i], slot, t)
                slot_of[pl_i] = slot
                kvcount[0] += 1
                pl_i += 1

            def load_slice(ts, b, rb):
                if ts < 4:
                    kb.dma(kb.act, hsem[b], hs[b][:, :], h_scr[rb + ts * 128:rb + (ts + 1) * 128, :], writes=[hsb[b]])
                else:
                    lo = max(rb - 8, 0)
                    hi = min(rb + T, ntok - 8)
                    kb.dma(kb.act, hsem[b], hs[b][0:8, :], h_scr[lo:lo + 8, :], writes=[hsb[b]])
                    kb.dma(kb.act, hsem[b], hs[b][8:16, :], h_scr[hi:hi + 8, :], writes=[])
                    hsb[b].w = kb.latest[hsem[b].key]

            def p2_stage_a(ts, rb):
                b = ts % 2
                np_ = 128 if ts < 4 else 16
                kb.op(kb.act, lambda e: e.activation(out=xn[b][0:np_, :], in_=hs[b][0:np_, :], func=AF.Square,
                                                     accum_out=ss[0:np_, ts:ts + 1]),
                      reads=[hsb[b]], writes=[xnb[b], ssb[ts]])
                kb.op(kb.act, lambda e: e.activation(out=ss[0:np_, ts:ts + 1], in_=ss[0:np_, ts:ts + 1], func=AF.Sqrt,
                                                     scale=1.0 / D, bias=mhalf[0:np_, 1:2]),
                      reads=[ssb[ts], m_buf], writes=[ssb[ts]])
                kb.op(kb.dve, lambda e: e.reciprocal(out=rstd[0:np_, ts:ts + 1], in_=ss[0:np_, ts:ts + 1]),
                      reads=[ssb[ts]], writes=[rstdb[ts]])
                kb.op(kb.dve, lambda e: e.scalar_tensor_tensor(out=xn[b][0:np_, :], in0=hs[b][0:np_, :],
                                                               scalar=rstd[0:np_, ts:ts + 1], in1=gm[0:np_, :],
                                                               op0=ALU.mult, op1=ALU.mult),
                      reads=[hsb[b], rstdb[ts], gmb], writes=[xnb[b]])
                if ts + 2 < 5:
                    load_slice(ts + 2, b, rb)

            def p2_stage_b(ts):
                b = ts % 2
                np_ = 128 if ts < 4 else 16
                for half in range(2):
                    bk = 4 + 2 * b + half
                    pv = banks[bk].bitcast(BF16)
                    mms = []
                    for k in range(8):
                        dc = half * 8 + k
                        mms.append((pv[:, k * 128:k * 128 + np_], xn[b][0:np_, dc * 128:(dc + 1) * 128],
                                    ident[0:np_, 0, 0:np_]))
                    kb.mm_group(mms, reads=[xnb[b], c_buf], writes=[bbuf[bk]], transpose=True)
                    src = pv[:, :].rearrange("p (k t) -> p k t", t=128)[:, :, 0:np_]
                    if ts < 4:
                        dst = uT[:, half * 8:half * 8 + 8, ts * 128:(ts + 1) * 128]
                        wb_ = uTb[ts * 2 + half]
                    else:
                        dst = uTh[:, half * 8:half * 8 + 8, :]
                        wb_ = uThb
                    if half == 0:
                        kb.op(kb.act, lambda e: e.activation(out=dst, in_=src, func=AF.Copy), reads=[bbuf[bk]], writes=[wb_])
                    else:
                        kb.op(kb.dve, lambda e: e.tensor_copy(dst, src), reads=[bbuf[bk]], writes=[wb_])

            if t == 0:
                load_slice(0, 0, rbase)
                load_slice(1, 1, rbase)
                p2_stage_a(0, rbase)
                for ts in range(5):
                    if ts + 1 < 5:
                        p2_stage_a(ts + 1, rbase)
                    p2_stage_b(ts)

            for i_ in range(2):
                kb.op(kb.pool, lambda e: e.memset(xn[i_][0:64, :], 0.0), writes=[xnb[i_]])
            for h in range(NH):
                kb.deps(kb.act, [xnb[h // 4], qTb[h]], [qaugb[h], xaugb[h]])
                kb.act.e.dma_start(out=qT[64:72, h, :], in_=qa_in[h, :, rbase:rbase + T]).then_inc(qasem.sem, 16)
                kb.act.e.dma_start(out=xn[h // 4][32:40, (h % 4) * T:(h % 4 + 1) * T],
                                   in_=qa_in[h, :, rbase:rbase + T]).then_inc(qasem.sem, 16)
                qasem.cnt += 32
            qtok = Tok(qasem.sem, qasem.cnt, qasem.key)
            kb.latest[qasem.key] = qtok
            for h in range(NH):
                kb.commit(qtok, [xnb[h // 4]], [qaugb[h], xaugb[h]])
            for pi in range(2):
                p = 6 + pi
                wv, wb = stream.get(W["in"], p)
                for gl in range(2):
                    g = 2 * pi + gl
                    w = 2 << g
                    ib = g % 2
                    kb.dma(kb.act, icsem[ib], icnt[ib][:], icnt_in[g:g + 1, rbase:rbase + T].broadcast_to([128, T]),
                           writes=[icb[ib]])
                    for cc in range(2):
                        c = 2 * gl + cc
                        bk = c % 4
                        kb.mm_group([(banks[bk][:], wv[:, k, c * 128:(c + 1) * 128], uT[:, k, :], k == 0, k == 15) for k in range(16)],
                                    reads=[wb] + uTb, writes=[bbuf[bk]])
                        kb.op(kb.act, lambda e: e.activation(out=pin[:, cc, 8:8 + T], in_=banks[bk][:], func=AF.Copy),
                              reads=[bbuf[bk]], writes=[pinb])
                        kb.mm_group([(banks[bk][:, 0:16], wv[:, k, c * 128:(c + 1) * 128], uTh[:, k, :], k == 0, k == 15) for k in range(16)],
                                    reads=[wb, uThb], writes=[bbuf[bk]])
                        kb.op(kb.dve, lambda e: e.tensor_scalar(out=pin[:, cc, 0:8], in0=banks[bk][:, 0:8],
                                                                scalar1=hmask[:, 2 * t:2 * t + 1], scalar2=None, op0=ALU.mult),
                              reads=[bbuf[bk], hmb], writes=[pinb])
                        kb.op(kb.dve, lambda e: e.tensor_scalar(out=pin[:, cc, 8 + T:16 + T], in0=banks[bk][:, 8:16],
                                                                scalar1=hmask[:, 2 * t + 1:2 * t + 2], scalar2=None, op0=ALU.mult),
                              reads=[bbuf[bk], hmb], writes=[pinb])
                    N = T + 16
                    kb.op(kb.pool, lambda e: e.tensor_tensor(out=TA[:, :, 1:N], in0=pin[:, :, 0:N - 1], in1=pin[:, :, 1:N], op=ALU.add),
                          reads=[pinb], writes=[TAb])
                    cur, curb, oth, othb = TA, TAb, TB, TBb
                    sh = 1
                    lo, hi = 1, N
                    for step in range(g):
                        nlo, nhi = lo + sh, hi - sh
                        kb.op(kb.pool, lambda e: e.tensor_tensor(out=oth[:, :, nlo:nhi], in0=cur[:, :, nlo - sh:nhi - sh],
                                                                 in1=cur[:, :, nlo + sh:nhi + sh], op=ALU.add),
                              reads=[curb], writes=[othb])
                        cur, curb, oth, othb = oth, othb, cur, curb
                        lo, hi = nlo, nhi
                        sh *= 2
                    assert lo <= 8 and hi >= 8 + T, (g, lo, hi)
                    pooled = pooledT[pi]
                    for cc in range(2):
                        c = 2 * gl + cc
                        kb.op(kb.pool, lambda e: e.tensor_tensor(out=oth[:, cc, 8:8 + T], in0=cur[:, cc, 8:8 + T], in1=icnt[ib][:],
                                                                 op=ALU.mult),
                              reads=[curb, icb[ib]], writes=[othb])
                        kb.op(kb.pool, lambda e: e.tensor_tensor(out=pooled[:, c * T:(c + 1) * T], in0=oth[:, cc, 8:8 + T],
                                                                 in1=pin[:, cc, 8:8 + T], op=ALU.subtract),
                              reads=[othb, pinb], writes=[pldb[pi]])
                stream.done()
                p = pi
                wv, wb = stream.get(W["in"], p)
                for c in range(4):
                    h = 4 * pi + c
                    bk = h % 4
                    kb.mm_group([(banks[bk][:], wv[:, k, c * 128:(c + 1) * 128], uT[:, k, :], k == 0, k == 15) for k in range(16)],
                                reads=[wb] + uTb, writes=[bbuf[bk]])
                    kb.op(kb.act, lambda e: e.activation(out=qT[0:64, h, :], in_=banks[bk][0:64, :], func=AF.Copy),
                          reads=[bbuf[bk]], writes=[qTb[h]])
                    kb.op(kb.dve, lambda e: e.tensor_copy(xn[pi][64:128, c * T:(c + 1) * T], banks[bk][64:128, :]),
                          reads=[bbuf[bk]], writes=[xnb[pi]])
                stream.done()

                pooled = pooledT[pi]
                for gl in range(2):
                    g = 2 * pi + gl
                    for oc in range(2):
                        bk = (2 * g + oc) % 4
                        kb.mm_group([(banks[bk][:], pgw[:, 2 * g + k, oc * 128:(oc + 1) * 128],
                                      pooled[:, (2 * gl + k) * T:(2 * gl + k + 1) * T], k == 0, k == 1) for k in range(2)],
                                    reads=[pgb, pldb[pi]], writes=[bbuf[bk]])
                        ci = 2 * g + oc
                        kb.op(kb.dve, lambda e: e.tensor_scalar(out=ppT[:, ci, :], in0=banks[bk][:], scalar1=psc[:, ci:ci + 1],
                                                                scalar2=None, op0=ALU.mult),
                              reads=[bbuf[bk], pscb], writes=[ppTb[ci]])

            nparts = (nkg + HK - 1) // HK
            blocks = []
            for h in range(NH):
                for part in range(nparts):
                    ph, kt0, n = plan[h * nparts + part]
                    assert ph == h
                    for i in range(n):
                        blocks.append((h, h * nparts + part, i, kt0 + i, i == n - 1,
                                       part == 0 and i == 0, part == nparts - 1 and i == n - 1))
            NB = len(blocks)

            usedP = {}

            def emit_qk(g):
                h, pix, i, kt, _, fh, _ = blocks[g]
                slot = slot_of[pix]
                kpos = (kt - k0g) * 128
                qpos = rbase - k0g * 128
                diag = not (kpos + 128 <= qpos or kpos >= qpos + T)
                di = (kpos - qpos) // 128
                g_ = g % 2
                ops = ((kx[slot], qT[:, h, :], [kvb[slot], qTb[h], qaugb[h]]),
                       (ky[slot], xn[h // 4][:, (h % 4) * T:(h % 4 + 1) * T], [kvb[slot], xnb[h // 4], xaugb[h]]))
                for c in range(2):
                    bk = g_ * 2 + c
                    kt_, qz_, rd = ops[c]
                    mms = [(banks[bk][:], kt_[:, i * 128:(i + 1) * 128], qz_, True, True)]
                    kb.mm_group(mms, reads=rd, writes=[bbuf[bk]])
                    if diag:
                        kb.op(kb.dve, lambda e: e.scalar_tensor_tensor(out=banks[bk][:], in0=dbias[:, 384 - 128 * di:896 - 128 * di],
                                                                       scalar=-8.0 * SLOPES[h], in1=banks[bk][:],
                                                                       op0=ALU.mult, op1=ALU.add),
                              reads=[dbb, bbuf[bk]], writes=[bbuf[bk]])
                    kb.op(kb.act, lambda e: e.activation(out=Et[g_][c][:], in_=banks[bk][:], func=AF.Exp, scale=QK_SCALE),
                          reads=[bbuf[bk]], writes=[Eb[g_][c]])

            def emit_pv(g):
                h, pix, i, kt, _, fh, lh = blocks[g]
                slot = slot_of[pix]
                g_ = g % 2
                for c in range(2):
                    kb.mm_group([(banks[4 + c][:], vbuf[slot][:, i, :], Et[g_][c][:], fh, lh)],
                                reads=[kvb[slot], Eb[g_][c]], writes=[bbuf[4 + c]] if fh else [])
                kb.mm_group([(banks[6][:], onesb[:], Et[g_][0][:], fh, lh)],
                            reads=[m_buf, Eb[g_][0]], writes=[bbuf[6]] if fh else [])
                j = hblk[g] % 2
                acc, accb = acc2[h % 2], accb2[h % 2]
                eng = kb.dve if j == 0 else kb.pool
                if hblk[g] < 2:
                    kb.op(eng, lambda e: e.tensor_copy(acc[j][:], Et[g_][1][:]), reads=[Eb[g_][1]], writes=[accb[j]])
                    if j == 1:
                        usedP[h] = True
                else:
                    kb.op(eng, lambda e: e.tensor_tensor(out=acc[j][:], in0=acc[j][:], in1=Et[g_][1][:], op=ALU.add),
                          reads=[Eb[g_][1], accb[j]], writes=[accb[j]])

            def evac_head(h):
                for bk in (4, 5, 6):
                    bbuf[bk].w = kb.latest[kb.pe.key]
                kb.op(kb.act, lambda e: e.activation(out=r0[:], in_=banks[4][:], func=AF.Copy), reads=[bbuf[4]], writes=[r0b])
                kb.op(kb.dve, lambda e: e.tensor_copy(r1[:], banks[5][:]), reads=[bbuf[5]], writes=[r1b])
                kb.op(kb.dve, lambda e: e.tensor_copy(rl0[:], banks[6][:]), reads=[bbuf[6]], writes=[rl0b])

            def evac_steps(h):
                st_ = []
                acc, accb = acc2[h % 2], accb2[h % 2]
                up = usedP.get(h, False)

                def l1_():
                    mm1 = [(banks[7][:], ones32[:], acc[0][:], True, not up)]
                    if up:
                        mm1.append((banks[7][:], ones32[:], acc[1][:], False, True))
                    kb.mm_group(mm1, reads=[m_buf, accb[0], accb[1]], writes=[bbuf[7]])
                st_.append(l1_)

                def rq_(tile_ap, buf, q):
                    return lambda: kb.op(kb.dve, lambda e: e.reciprocal(out=tile_ap[:, q * 128:(q + 1) * 128],
                                                                        in_=tile_ap[:, q * 128:(q + 1) * 128]),
                                         reads=[buf], writes=[buf])
                for q in range(4):
                    st_.append(("heavy", rq_(rl0, rl0b, q)))
                st_.append(lambda: kb.op(kb.dve, lambda e: e.tensor_tensor(out=r0[:], in0=r0[:], in1=rl0[:], op=ALU.mult),
                                         reads=[rl0b, r0b], writes=[r0b]))
                for q in range(4):
                    st_.append(("heavy", rq_(banks[7], bbuf[7], q)))
                st_.append(lambda: kb.op(kb.dve, lambda e: e.tensor_tensor(out=r1[:], in0=r1[:], in1=banks[7][:], op=ALU.mult),
                                         reads=[bbuf[7], r1b], writes=[r1b]))
                st_.append(lambda: kb.op(kb.dve, lambda e: e.scalar_tensor_tensor(out=r0[:], in0=r1[:], scalar=neglam, in1=r0[:],
                                                                               op0=ALU.mult, op1=ALU.add),
                                         reads=[r1b, r0b, smb], writes=[r0b]))
                st_.append(lambda: kb.op(kb.act, lambda e: e.activation(out=sq[:], in_=r0[:], func=AF.Square), reads=[r0b], writes=[sqb]))
                st_.append(lambda: kb.mm_group([(banks[7][:], onesb[:], sq[:], True, True)], reads=[m_buf, sqb], writes=[bbuf[7]]))
                st_.append(lambda: kb.op(kb.dve, lambda e: e.tensor_scalar(out=r1[:], in0=banks[7][:], scalar1=1.0 / 128, scalar2=EPS,
                                                                        op0=ALU.mult, op1=ALU.add),
                                         reads=[bbuf[7]], writes=[r1b]))
                st_.append(lambda: kb.op(kb.act, lambda e: e.activation(out=r1[:], in_=r1[:], func=AF.Ln), reads=[r1b], writes=[r1b]))
                st_.append(lambda: kb.op(kb.act, lambda e: e.activation(out=r1[:], in_=r1[:], func=AF.Exp, scale=-0.5),
                                         reads=[r1b], writes=[r1b]))
                st_.append(lambda: kb.op(kb.dve, lambda e: e.scalar_tensor_tensor(out=aT[:, h, :], in0=r0[:], scalar=gsc, in1=r1[:],
                                                                               op0=ALU.mult, op1=ALU.mult),
                                         reads=[r0b, r1b, smb], writes=[aTb[h]]))
                return st_

            hblk = []
            cnt_ = {}
            for (h_, *_r) in blocks:
                hblk.append(cnt_.get(h_, 0))
                cnt_[h_] = cnt_.get(h_, 0) + 1
            emit_qk(0)
            steps = []
            for g in range(NB):
                if g + 1 < NB:
                    emit_qk(g + 1)
                emit_pv(g)
                h, pix, i, kt, plast, fh, lh = blocks[g]
                if plast:
                    slot = slot_of[pix]
                    if pl_i < len(plan):
                        load_kv(*plan[pl_i], slot, t)
                        slot_of[pl_i] = slot
                        kvcount[0] += 1
                        pl_i += 1
                def run_step():
                    st1 = steps.pop(0)
                    if isinstance(st1, tuple):
                        st1[1]()
                        return True
                    st1()
                    return False

                if lh:
                    while steps:
                        run_step()
                    evac_head(h)
                    steps = evac_steps(h)
                elif steps:
                    heavy = run_step()
                    if steps and not heavy:
                        run_step()
            while steps:
                st1 = steps.pop(0)
                (st1[1] if isinstance(st1, tuple) else st1)()

            gt = ((acc2[0][0], accb2[0][0]), (acc2[0][1], accb2[0][1]))
            for mg in range(4):
                for pas, (gp_, wsn, src_, srcb_) in enumerate(((12 + mg, "pp", ppT, ppTb), (8 + mg, "ap", aT, aTb))):
                    wg_, wgb = stream.get(W["in"], gp_)
                    stream.nuse += 1
                    wa_, wab = stream.get(W[wsn], mg)
                    stream.nuse -= 1
                    for c in range(4):
                        m = 4 * mg + c
                        b0 = (m % 2) * 2
                        kb.mm_group([(banks[b0][:], wg_[:, k, c * 128:(c + 1) * 128], uT[:, k, :], k == 0, k == 15) for k in range(16)],
                                    reads=[wgb] + uTb, writes=[bbuf[b0]])
                        kb.mm_group([(banks[b0 + 1][:], wa_[:, k, c * 128:(c + 1) * 128], src_[:, k, :], k == 0, k == 7) for k in range(8)],
                                    reads=[wab] + srcb_, writes=[bbuf[b0 + 1]])
                        rr, rrb = gt[m % 2]
                        kb.op(kb.act, lambda e: e.activation(out=rr[:], in_=banks[b0][:], func=AF.Sigmoid), reads=[bbuf[b0]], writes=[rrb])
                        if pas == 0:
                            kb.op(kb.dve, lambda e: e.tensor_tensor(out=mg_t[:, m, :], in0=rr[:], in1=banks[b0 + 1][:], op=ALU.mult),
                                  reads=[rrb, bbuf[b0 + 1]], writes=[mgb[m]])
                        else:
                            kb.op(kb.dve, lambda e: e.tensor_tensor(out=rr[:], in0=rr[:], in1=banks[b0 + 1][:], op=ALU.mult),
                                  reads=[rrb, bbuf[b0 + 1]], writes=[rrb])
                            kb.op(kb.dve, lambda e: e.tensor_tensor(out=mg_t[:, m, :], in0=rr[:], in1=mg_t[:, m, :], op=ALU.add),
                                  reads=[rrb, mgb[m]], writes=[mgb[m]])
                    stream.done()
                    stream.done()

            nxt = (t + 1) * T if t + 1 < nt else None
            if nxt is not None:
                load_slice(0, 0, nxt)
                load_slice(1, 1, nxt)
                p2_stage_a(0, nxt)
            for dp in range(4):
                wv, wb = stream.get(W["o"], dp)
                blks = []
                for ts in range(4):
                    n_ = (dp * 4 + ts) % len(hblk_t)
                    blks.append(n_)
                    kb.dma(kb.pool, hbsem[n_], hblk_t[n_][:], h_scr[rbase + ts * 128:rbase + (ts + 1) * 128, dp * T:(dp + 1) * T],
                           writes=[hblk_b[n_]])
                for ts in range(4):
                    bk = ts
                    n_ = blks[ts]
                    kb.mm_group([(banks[bk][:], mg_t[:, k, ts * 128:(ts + 1) * 128], wv[:, k, :], k == 0, k == 15) for k in range(16)],
                                reads=[wb] + mgb, writes=[bbuf[bk]])
                    kb.op(kb.dve, lambda e: e.tensor_tensor(out=hblk_t[n_][:], in0=banks[bk][:], in1=hblk_t[n_][:], op=ALU.add),
                          reads=[bbuf[bk], hblk_b[n_]], writes=[hblk_b[n_]])
                    kb.dma(kb.pool, hbsem[n_], h2_scr[rbase + ts * 128:rbase + (ts + 1) * 128, dp * T:(dp + 1) * T], hblk_t[n_][:],
                           reads=[hblk_b[n_]])
                stream.done()
                if nxt is not None:
                    p2_stage_a(dp + 1, nxt)
                    p2_stage_b(dp)
            if nxt is not None:
                p2_stage_b(4)
        kb.barrier()

    if dbg == "p2a":
        dsem = DmaSem(kb, "dbg")
        kb.dma(kb.sp, dsem, y_out[:, :], h2_scr[:, :])
        kb.sp.wait(kb.latest["dbg"])
        return kb

    with ExitStack() as st:
        seq = []
        for t in range(nt):
            seq += [(W["gu1"], p) for p in range(22)] + [(W["dn1"], p) for p in range(12)]
        stream = WStream(kb, st, seq, "c")
        xh = kb.sbuf(st, "xh3", [128, 4, D], F32)
        xhb = [Buf(f"xh3{ts}") for ts in range(4)]
        xsem = [DmaSem(kb, f"p3x{ts}") for ts in range(4)]
        cb3 = ConstBatch(kb, c_sem)
        g1 = kb.sbuf(st, "g_ffn2", [128, D], F32)
        g1b = cb3.load(kb.act, g1[:], norms["ffn2_norm"][0:1, :].broadcast_to([128, D]))
        g2 = kb.sbuf(st, "g_fin", [128, D], F32)
        g2b = cb3.load(kb.act, g2[:], norms["final_norm"][0:1, :].broadcast_to([128, D]))
        cb3.finish()
        uT = kb.sbuf(st, "uT3", [128, 16, T], BF16)
        uTb = [Buf(f"uT3{i}") for i in range(8)]
        actT = kb.sbuf(st, "actT3", [128, NJ, T], BF16)
        actTb = [Buf(f"actT3{j}") for j in range(NJ)]
        sg = [kb.sbuf(st, f"sg3{i}", [128, T], F32) for i in range(2)]
        sgb = [Buf(f"sg3{i}") for i in range(2)]
        junk = kb.sbuf(st, "junk3", [128, D], BF16)
        ss = kb.sbuf(st, "ss3", [128, 4], F32)
        rstd = kb.sbuf(st, "rstd3", [128, 4], F32)
        xn = [kb.sbuf(st, f"xn3{i}", [128, D], BF16) for i in range(2)]
        junkb = Buf("junk3")
        ssb = [Buf(f"ss3{i}") for i in range(4)]
        rstdb = [Buf(f"rstd3{i}") for i in range(4)]
        tmp = (junk, junkb, ss, ssb, rstd, rstdb, xn, [Buf("xn30"), Buf("xn31")])

        ostage = [kb.sbuf(st, f"ostage{i}", [128, D], F32) for i in range(2)]
        ostb = [Buf("ost0"), Buf("ost1")]
        osem = [DmaSem(kb, "ost0"), DmaSem(kb, "ost1")]

        def load_x3(t):
            for ts in range(4):
                kb.dma(kb.act, xsem[ts], xh[:, ts, :], h2_scr[t * T + ts * 128:t * T + (ts + 1) * 128, :], writes=[xhb[ts]])

        pf = [kb.sbuf(st, f"pf{i}", [128, D], F32) for i in range(2)]
        pfb = [Buf("pf0"), Buf("pf1")]
        pfsem = [DmaSem(kb, "pf0"), DmaSem(kb, "pf1")]
        ssP = kb.sbuf(st, "ssP", [128, 4], F32)
        rstdP = kb.sbuf(st, "rstdP", [128, 4], F32)
        ssPb = [Buf(f"ssP{i}") for i in range(4)]
        rstdPb = [Buf(f"rstdP{i}") for i in range(4)]
        xn3 = tmp[6]
        xn3b = tmp[7]

        def pf_load(ts, tn):
            b = ts % 2
            kb.dma(kb.act, pfsem[b], pf[b][:], h2_scr[tn * T + ts * 128:tn * T + (ts + 1) * 128, :], writes=[pfb[b]])

        def pf_a(ts, tn):
            b = ts % 2
            kb.op(kb.act, lambda e: e.activation(out=junk[:], in_=pf[b][:], func=AF.Square, accum_out=ssP[:, ts:ts + 1]),
                  reads=[pfb[b]], writes=[junkb, ssPb[ts]])
            kb.op(kb.act, lambda e: e.activation(out=ssP[:, ts:ts + 1], in_=ssP[:, ts:ts + 1], func=AF.Sqrt,
                                                 scale=1.0 / D, bias=mhalf[:, 1:2]),
                  reads=[ssPb[ts], m_buf], writes=[ssPb[ts]])
            kb.op(kb.dve, lambda e: e.reciprocal(out=rstdP[:, ts:ts + 1], in_=ssP[:, ts:ts + 1]),
                  reads=[ssPb[ts]], writes=[rstdPb[ts]])
            kb.op(kb.dve, lambda e: e.scalar_tensor_tensor(out=xn3[b][:], in0=pf[b][:], scalar=rstdP[:, ts:ts + 1],
                                                           in1=g1[:], op0=ALU.mult, op1=ALU.mult),
                  reads=[pfb[b], rstdPb[ts], g1b], writes=[xn3b[b]])
            if ts + 2 < 4:
                pf_load(ts + 2, tn)

        def pf_b(ts):
            b = ts % 2
            for half in range(2):
                bk = 2 * b + half
                pv = banks[bk].bitcast(BF16)
                mms = []
                for k in range(8):
                    dc = half * 8 + k
                    mms.append((pv[:, k * 128:(k + 1) * 128], xn3[b][:, dc * 128:(dc + 1) * 128], ident[:, 0, :]))
                kb.mm_group(mms, reads=[xn3b[b], c_buf], writes=[bbuf[bk]], transpose=True)
                src = pv[:, :].rearrange("p (k t) -> p k t", t=128)
                dst = uT[:, half * 8:half * 8 + 8, ts * 128:(ts + 1) * 128]
                ub = uTb[ts * 2 + half]
                if half == 0:
                    kb.op(kb.act, lambda e: e.activation(out=dst, in_=src, func=AF.Copy), reads=[bbuf[bk]], writes=[ub])
                else:
                    kb.op(kb.dve, lambda e: e.tensor_copy(dst, src), reads=[bbuf[bk]], writes=[ub])

        load_x3(0)
        for t in range(nt):
            rbase = t * T
            if t == 0:
                rmsnorm_to_uT(xh, xhb, g1, g1b, uT, uTb, tmp)
            cb_ = None
            if t + 1 < nt:
                tn_ = t + 1

                def cb_(dp, tn_=tn_):
                    if dp == 0:
                        pf_load(0, tn_)
                        pf_load(1, tn_)
                        pf_a(0, tn_)
                        pf_a(1, tn_)
                    elif dp == 1:
                        pf_b(0)
                        pf_a(2, tn_)
                    elif dp == 2:
                        pf_b(1)
                        pf_a(3, tn_)
                    else:
                        pf_b(2)
                        pf_b(3)
            ffn(stream, W["gu1"], W["dn1"], uT, uTb, actT, actTb, xh, xhb, sg, sgb, after_dp=cb_)
            for ts in range(4):
                ob = ts % 2
                kb.op(kb.act, lambda e: e.activation(out=junk[:], in_=xh[:, ts, :], func=AF.Square, accum_out=ss[:, ts:ts + 1]),
                      reads=[xhb[ts]], writes=[junkb, ssb[ts]])
                kb.op(kb.act, lambda e: e.activation(out=ss[:, ts:ts + 1], in_=ss[:, ts:ts + 1], func=AF.Sqrt,
                                                     scale=1.0 / D, bias=mhalf[:, 1:2]),
                      reads=[ssb[ts], m_buf], writes=[ssb[ts]])
                kb.op(kb.dve, lambda e: e.reciprocal(out=rstd[:, ts:ts + 1], in_=ss[:, ts:ts + 1]),
                      reads=[ssb[ts]], writes=[rstdb[ts]])
                kb.op(kb.dve, lambda e: e.scalar_tensor_tensor(out=ostage[ob][:], in0=xh[:, ts, :], scalar=rstd[:, ts:ts + 1],
                                                               in1=g2[:], op0=ALU.mult, op1=ALU.mult),
                      reads=[xhb[ts], rstdb[ts], g2b], writes=[ostb[ob]])
                kb.dma(kb.act, osem[ob], y_out[rbase + ts * 128:rbase + (ts + 1) * 128, :], ostage[ob][:], reads=[ostb[ob]])
                if t + 1 < nt:
                    kb.dma(kb.act, xsem[ts], xh[:, ts, :], h2_scr[(t + 1) * T + ts * 128:(t + 1) * T + (ts + 1) * 128, :],
                           writes=[xhb[ts]])
        kb.barrier()
    return kb


def host_tables(seqlens, nA):
    bf = ml_dtypes.bfloat16
    ntok = int(sum(seqlens))
    nt = ntok // T
    pos = np.concatenate([np.arange(L) for L in seqlens]).astype(np.int64)
    slen = np.concatenate([np.full(L, L) for L in seqlens]).astype(np.int64)
    sflag = np.zeros(ntok, np.float32)
    start = 0
    for gi, (g0, g1) in enumerate(((0, nA * T), (nA * T, ntok))):
        k = 0
        st = 0
        for L in seqlens:
            if st >= g0 and st < g1:
                sflag[st:st + L] = k
                k += 1
            st += L
    assert sflag.max() <= 1
    a = (pos // 64).astype(np.float32)
    b = (pos % 64).astype(np.float32)
    one = np.ones(ntok, np.float32)
    zero = np.zeros(ntok, np.float32)
    ka1 = np.stack([one, sflag, one, a, one, b, zero, zero])
    ka = np.stack([ka1, -ka1, np.zeros_like(ka1)]).astype(bf)
    qa = np.zeros((NH, 8, ntok), np.float32)
    for h in range(NH):
        m8 = 8.0 * SLOPES[h]
        qa[h] = np.stack([-m8 * BIGPOS * sflag, m8 * BIGPOS * one, -m8 * 64 * a, m8 * 64 * one, -m8 * b, m8 * one, zero, zero])
    qa = qa.astype(bf)
    kk = np.arange(128)[:, None]
    jj = np.arange(896)[None, :]
    dbias = np.abs(jj - 384 - kk).astype(np.float32)
    sid = np.zeros((128, NH + 1, 128), np.float32)
    sid[:, 0] = np.eye(128)
    for h in range(NH):
        sid[:, 1 + h] = -8.0 * SLOPES[h] * np.eye(128)
    sid = sid.astype(bf)
    icnt = np.zeros((4, ntok), np.float32)
    for g, w in enumerate((2, 4, 8, 16)):
        lo = np.clip(pos - w // 2, 0, slen - 1)
        hi = np.clip(pos + w // 2 - 1, 0, slen - 1)
        icnt[g] = 1.0 / (hi - lo + 1).astype(np.float32)
    hmask = np.zeros((128, 2 * nt), np.float32)
    for t in range(nt):
        hmask[:, 2 * t] = 0.0 if pos[t * T] == 0 else 1.0
        hmask[:, 2 * t + 1] = 0.0 if pos[t * T + T - 1] == slen[t * T + T - 1] - 1 else 1.0
    return {"ka_tab": ka, "qa_tab": qa, "dbias_tab": dbias, "sident_tab": sid, "icnt_tab": icnt, "hmask_tab": hmask}


def weight_inputs(inp):
    im = {}
    for i in (1, 2):
        im[f"ffn{i}_w_gu"] = np.ascontiguousarray(inp[f"ffn{i}_w_gu"][0])
        im[f"ffn{i}_w_down"] = np.ascontiguousarray(inp[f"ffn{i}_w_down"][0])
    for n in ("w_in", "w_attn_proj", "w_pool_proj", "w_pool_grp", "w_out"):
        im[n] = np.ascontiguousarray(inp[n][0])
    for n in ("ffn1_norm", "mix_norm", "ffn2_norm"):
        im[n] = np.ascontiguousarray(inp[n]).reshape(1, D)
    im["final_norm"] = np.ascontiguousarray(inp["final_norm"]).reshape(1, D)
    im["lam_vecs"] = np.concatenate([inp["lambda_q1"][0], inp["lambda_k1"][0], inp["lambda_q2"][0],
                                     inp["lambda_k2"][0]]).reshape(1, 256).astype(np.float32)
    im["attn_subln_g"] = np.ascontiguousarray(inp["attn_subln_g"][0]).reshape(128, 1)
    im["pool_scale"] = np.ascontiguousarray(inp["pool_scale"][0].reshape(8, 128).T)
    return im


_PROGRAM = {}


def kernel(**inputs):
    inp = {k: np.asarray(v) for k, v in inputs.items()}
    xp = inp["x_prompt"]
    xs = inp["x_sample"]
    if "full" not in _PROGRAM:
        _PROGRAM["full"] = build_program(nt=12, nA=8)
    kb = _PROGRAM["full"]
    wim = weight_inputs(inp)
    tabs_a = host_tables([4096, 2048], 8)
    tabs_b = host_tables([2048, 2048, 2048], 8)
    in_maps = []
    for c in range(8):
        if c < 4:
            x = np.concatenate([xs[c], xp[c]], axis=0)
            tabs = tabs_a
        else:
            p0 = 4 + 3 * (c - 4)
            x = np.concatenate([xp[p0], xp[p0 + 1], xp[p0 + 2]], axis=0)
            tabs = tabs_b
        im = dict(wim)
        im.update(tabs)
        im["x"] = np.ascontiguousarray(x, dtype=np.float32)
        in_maps.append(im)
    res = run_bass_kernel_spmd(kb.nc, in_maps, core_ids=list(range(8)))
    y_prompt = np.zeros((16, 2048, D), np.float32)
    y_sample = np.zeros((4, 4096, D), np.float32)
    for c in range(8):
        y = np.asarray(res.results[c]["y"])
        if c < 4:
            y_sample[c] = y[0:4096]
            y_prompt[c] = y[4096:6144]
        else:
            p0 = 4 + 3 * (c - 4)
            for i in range(3):
                y_prompt[p0 + i] = y[2048 * i:2048 * (i + 1)]
    return (y_prompt, y_sample)
```

```python
import numpy as np
import ml_dtypes
from contextlib import ExitStack

import concourse.bass as bass
import concourse.mybir as mybir
from concourse.bass_utils import run_bass_kernel_spmd

F32 = mybir.dt.float32
BF16 = mybir.dt.bfloat16
AF = mybir.ActivationFunctionType
ALU = mybir.AluOpType

D = 2048
DFF = 5632
NJ = DFF // 128
T = 512
NH = 8
EPS = 1e-6
LAMBDA_INIT = 0.8 - 0.6 * 1.0
SLOPES = [2.0 ** (-(h + 1)) for h in range(NH)]
QK_SCALE = 0.125
BIGPOS = float(2 ** 20)
WSLOT_ELEMS = 16 * 512
NWSLOT = 3


class Tok:
    __slots__ = ("sem", "val", "key")

    def __init__(self, sem, val, key):
        self.sem, self.val, self.key = sem, val, key


class Buf:
    __slots__ = ("name", "w", "r")

    def __init__(self, name):
        self.name = name
        self.w = None
        self.r = []


class Eng:
    def __init__(self, kb, e, name):
        self.kb, self.e, self.name = kb, e, name
        self.sem = kb.new_sem("e_" + name)
        self.key = "e_" + name
        self.cnt = 0
        self.seen = {}

    def wait(self, tok):
        if tok is None:
            return
        if self.name == "pe" and tok.key == self.key:
            return
        if self.seen.get(tok.key, 0) >= tok.val:
            return
        self.e.wait_ge(tok.sem, tok.val)
        self.seen[tok.key] = tok.val

    def signal(self, ins):
        self.cnt += 1
        ins.then_inc(self.sem, 1)
        t = Tok(self.sem, self.cnt, self.key)
        self.kb.latest[self.key] = t
        return t


class DmaSem:
    def __init__(self, kb, name):
        self.sem = kb.new_sem(name)
        self.key = name
        self.cnt = 0


class KB:
    def __init__(self, nt=12, ngroupA=8, dbg=None):
        self.nt = nt
        self.ntok = nt * T
        self.nA = min(ngroupA, nt)
        self.dbg = dbg
        self.nc = bass.Bass("TRN2", target_bir_lowering=False)
        self.stack = ExitStack()
        self.latest = {}
        self.nsem = 0
        nc = self.nc
        self.pe = Eng(self, nc.tensor, "pe")
        self.act = Eng(self, nc.scalar, "act")
        self.dve = Eng(self, nc.vector, "dve")
        self.pool = Eng(self, nc.gpsimd, "pool")
        self.sp = Eng(self, nc.sync, "sp")
        self.engs = [self.pe, self.act, self.dve, self.pool, self.sp]

    def new_sem(self, name):
        self.nsem += 1
        return self.stack.enter_context(self.nc.semaphore(name))

    def dram(self, name, shape, dt, kind):
        return self.nc.dram_tensor(name, list(shape), dt, kind=kind).ap()

    def sbuf(self, st, name, shape, dt):
        return st.enter_context(self.nc.sbuf_tensor(name, list(shape), dt))

    def deps(self, eng, reads, writes):
        for b in reads:
            eng.wait(b.w)
        for b in writes:
            eng.wait(b.w)
            for t in b.r:
                eng.wait(t)

    def commit(self, tok, reads, writes):
        for b in reads:
            b.r.append(tok)
            if len(b.r) > 24:
                d = {}
                for t in b.r:
                    if t.key not in d or d[t.key].val < t.val:
                        d[t.key] = t
                b.r = list(d.values())
        for b in writes:
            b.w = tok
            b.r = []

    def op(self, eng, fn, reads=(), writes=()):
        self.deps(eng, reads, writes)
        ins = fn(eng.e)
        tok = eng.signal(ins)
        self.commit(tok, reads, writes)
        return tok

    def mm_group(self, mms, reads=(), writes=(), transpose=False):
        eng = self.pe
        self.deps(eng, reads, writes)
        ins = None
        for m in mms:
            if transpose:
                ins = eng.e.transpose(m[0], m[1], m[2])
            else:
                ins = eng.e.matmul(m[0], m[1], m[2], start=m[3], stop=m[4])
        tok = eng.signal(ins)
        self.commit(tok, reads, writes)
        return tok

    def dma(self, eng, dsem, out, in_, reads=(), writes=(), n=1):
        self.deps(eng, reads, writes)
        eng.e.dma_start(out=out, in_=in_).then_inc(dsem.sem, 16)
        dsem.cnt += 16
        tok = Tok(dsem.sem, dsem.cnt, dsem.key)
        self.latest[dsem.key] = tok
        self.commit(tok, reads, writes)
        return tok

    def barrier(self, skip_prefix=None):
        toks = [t for k, t in self.latest.items() if not (skip_prefix and k.startswith(skip_prefix))]
        for e in self.engs:
            for t in toks:
                e.wait(t)


class ConstBatch:
    def __init__(self, kb, dsem):
        self.kb, self.dsem, self.bufs = kb, dsem, []

    def load(self, eng, out, in_):
        b = Buf("const")
        eng.e.dma_start(out=out, in_=in_).then_inc(self.dsem.sem, 16)
        self.dsem.cnt += 16
        self.bufs.append(b)
        return b

    def finish(self):
        tok = Tok(self.dsem.sem, self.dsem.cnt, self.dsem.key)
        self.kb.latest[self.dsem.key] = tok
        for b in self.bufs:
            b.w = tok
        self.bufs = []


class WeightSet:
    def __init__(self, kb, name, npanel, kc_list):
        self.name = name
        self.npanel = npanel
        self.kc = kc_list
        kcmax = max(kc_list)
        self.ap = kb.dram("wb_" + name, [npanel, 128, kcmax * 512], BF16, "Internal")
        self.cv = DmaSem(kb, "cv_" + name)
        self.tok = None
        self.ptok = {}


class WStream:
    def __init__(self, kb, st, seq, tag, nslot=NWSLOT):
        self.kb = kb
        self.seq = seq
        self.ns = nslot
        self.slots = [kb.sbuf(st, f"wslot{tag}{i}", [128, WSLOT_ELEMS], BF16) for i in range(nslot)]
        self.bufs = [Buf(f"wslot{i}") for i in range(nslot)]
        self.sems = [DmaSem(kb, f"wl{tag}{i}") for i in range(nslot)]
        self.nload = 0
        self.nuse = 0
        for _ in range(nslot):
            self._load()

    def _load(self):
        if self.nload >= len(self.seq):
            return
        ws, p = self.seq[self.nload]
        s = self.nload % self.ns
        kb = self.kb
        n = ws.kc[p] * 512
        kb.sp.wait(ws.ptok.get(p, ws.tok))
        kb.dma(kb.sp, self.sems[s], self.slots[s][:, 0:n], ws.ap[p, :, 0:n], writes=[self.bufs[s]])
        self.nload += 1

    def get(self, ws, p):
        assert self.seq[self.nuse] == (ws, p), (self.nuse, ws.name, p, self.seq[self.nuse][0].name, self.seq[self.nuse][1])
        s = self.nuse % self.ns
        kc = ws.kc[p]
        view = self.slots[s][:, 0:kc * 512].rearrange("p (k c) -> p k c", c=512)
        return view, self.bufs[s]

    def done(self):
        self.nuse += 1
        self._load()


def build_program(nt=12, nA=8, dbg=None):
    kb = KB(nt, nA, dbg)
    nc = kb.nc
    ntok = kb.ntok

    x_in = kb.dram("x", [ntok, D], F32, "ExternalInput")
    y_out = kb.dram("y", [ntok, D], F32, "ExternalOutput")
    w_ffn_gu = [kb.dram(f"ffn{i}_w_gu", [D, 2 * DFF], F32, "ExternalInput") for i in (1, 2)]
    w_ffn_dn = [kb.dram(f"ffn{i}_w_down", [DFF, D], F32, "ExternalInput") for i in (1, 2)]
    w_in = kb.dram("w_in", [D, 8192], F32, "ExternalInput")
    w_ap = kb.dram("w_attn_proj", [1024, D], F32, "ExternalInput")
    w_pp = kb.dram("w_pool_proj", [1024, D], F32, "ExternalInput")
    w_pg = kb.dram("w_pool_grp", [4, 256, 256], F32, "ExternalInput")
    w_o = kb.dram("w_out", [D, D], F32, "ExternalInput")
    norms = {n: kb.dram(n, [1, D], F32, "ExternalInput") for n in ("ffn1_norm", "mix_norm", "ffn2_norm", "final_norm")}
    lam_in = kb.dram("lam_vecs", [1, 256], F32, "ExternalInput")
    subln_in = kb.dram("attn_subln_g", [128, 1], F32, "ExternalInput")
    pscale_in = kb.dram("pool_scale", [128, 8], F32, "ExternalInput")
    ka_in = kb.dram("ka_tab", [3, 8, ntok], BF16, "ExternalInput")
    qa_in = kb.dram("qa_tab", [NH, 8, ntok], BF16, "ExternalInput")
    dbias_in = kb.dram("dbias_tab", [128, 896], F32, "ExternalInput")
    sident_in = kb.dram("sident_tab", [128, NH + 1, 128], BF16, "ExternalInput")
    icnt_in = kb.dram("icnt_tab", [4, ntok], F32, "ExternalInput")
    hmask_in = kb.dram("hmask_tab", [128, 2 * nt], F32, "ExternalInput")

    h_scr = kb.dram("h_scr", [ntok, D], F32, "Internal")
    h2_scr = kb.dram("h2_scr", [ntok, D], F32, "Internal")
    kt_scr = kb.dram("kt_scr", [NH, 128, ntok], BF16, "Internal")
    v_scr = kb.dram("v_scr", [NH, 128, ntok // 128, 128], BF16, "Internal")

    W = {}
    for i in (0, 1):
        W[f"gu{i}"] = WeightSet(kb, f"gu{i}", 22, [16] * 22)
        W[f"dn{i}"] = WeightSet(kb, f"dn{i}", 12, [16, 16, 12] * 4)
    W["in"] = WeightSet(kb, "in", 16, [16] * 16)
    W["ap"] = WeightSet(kb, "ap", 4, [8] * 4)
    W["pp"] = WeightSet(kb, "pp", 4, [8] * 4)
    W["o"] = WeightSet(kb, "o", 4, [16] * 4)
    pg_scr = kb.dram("wb_pg", [128, 8, 256], BF16, "Internal")
    pg_cv = DmaSem(kb, "cv_pg")

    def conv(ws, p, dst_c0, ncol, src, r0, kc, c0):
        dst = ws.ap[p, :, 0:kc * 512].rearrange("p (k c) -> p k c", c=512)[:, :, dst_c0:dst_c0 + ncol]
        s = src[r0:r0 + kc * 128, c0:c0 + ncol].rearrange("(k p) c -> p k c", p=128)
        kb.pool.e.dma_start(out=dst, in_=s).then_inc(ws.cv.sem, 16)
        ws.cv.cnt += 16

    def conv_finish(ws):
        ws.tok = Tok(ws.cv.sem, ws.cv.cnt, ws.cv.key)
        kb.latest[ws.cv.key] = ws.tok

    def conv_ffn(i, fine=False):
        g, d = W[f"gu{i}"], W[f"dn{i}"]
        base = g.cv
        for p in range(22):
            if fine and p % 3 == 0:
                g.cv = DmaSem(kb, f"cvf_gu{i}_{p}")
            conv(g, p, 0, 256, w_ffn_gu[i], 0, 16, 256 * p)
            conv(g, p, 256, 256, w_ffn_gu[i], 0, 16, DFF + 256 * p)
            if fine and (p % 3 == 2 or p == 21):
                conv_finish(g)
                for pp_ in range(p - p % 3, p + 1):
                    g.ptok[pp_] = g.tok
        if not fine:
            conv_finish(g)
        for dp in range(4):
            if fine:
                d.cv = DmaSem(kb, f"cvf_dn{i}_{dp}")
            for part, (j0, kc) in enumerate(((0, 16), (16, 16), (32, 12))):
                conv(d, dp * 3 + part, 0, 512, w_ffn_dn[i], j0 * 128, kc, dp * 512)
            if fine:
                conv_finish(d)
                for part in range(3):
                    d.ptok[dp * 3 + part] = d.tok
        if not fine:
            conv_finish(d)

    d0 = W["dn0"]
    for dp in range(4):
        d0.cv = DmaSem(kb, f"cvf_dn0_{dp}")
        for part, (j0, kc) in enumerate(((0, 16), (16, 16), (32, 12))):
            conv(d0, dp * 3 + part, 0, 512, w_ffn_dn[0], j0 * 128, kc, dp * 512)
        conv_finish(d0)
        for part in range(3):
            d0.ptok[dp * 3 + part] = d0.tok

    with ExitStack() as st0:
        NS = 4
        f32s = [kb.sbuf(st0, f"cvf{i}", [128, 16 * 512], F32) for i in range(NS)]
        b16s = [kb.sbuf(st0, f"cvb{i}", [128, 16 * 512], BF16) for i in range(NS)]
        fb = [Buf(f"cvf{i}") for i in range(NS)]
        bb3 = [[Buf(f"cvb{i}_{j}") for j in range(3)] for i in range(NS)]
        lds = [DmaSem(kb, f"cvld{i}") for i in range(NS)]
        sts = [DmaSem(kb, f"cvst{i}") for i in range(NS)]
        plist = []
        for p in range(22):
            plist.append((W["gu0"], p, 16, [(0, 256, w_ffn_gu[0], 0, 256 * p), (256, 256, w_ffn_gu[0], 0, DFF + 256 * p)]))
        for n_, (ws, p, kc, srcs) in enumerate(plist):
            i = n_ % NS
            nel = kc * 512
            fv = f32s[i][:, 0:nel].rearrange("p (k c) -> p k c", c=512)
            first = True
            for (dc0, ncol, src, r0_, c0_) in srcs:
                kb.dma(kb.sp, lds[i], fv[:, :, dc0:dc0 + ncol],
                       src[r0_:r0_ + kc * 128, c0_:c0_ + ncol].rearrange("(k p) c -> p k c", p=128),
                       writes=[fb[i]] if first else [])
                first = False
            fb[i].w = kb.latest[lds[i].key]
            cuts = [0, (nel * 3 // 8) // 512 * 512, nel]
            kb.op(kb.act, lambda e: e.activation(out=b16s[i][:, cuts[0]:cuts[1]], in_=f32s[i][:, cuts[0]:cuts[1]], func=AF.Copy),
                  reads=[fb[i]], writes=[bb3[i][0]])
            kb.op(kb.dve, lambda e: e.tensor_copy(b16s[i][:, cuts[1]:cuts[2]], f32s[i][:, cuts[1]:cuts[2]]),
                  reads=[fb[i]], writes=[bb3[i][1]])
            ws.ptok[p] = kb.dma(kb.sp, sts[i], ws.ap[p, :, 0:nel], b16s[i][:, 0:nel], reads=bb3[i][0:2])
        kb.barrier(skip_prefix=("cv_", "cvf_"))

    w_in_ws = W["in"]
    w_in_ws.cv = DmaSem(kb, "cvf_in_kv")
    for p in (2, 3, 4, 5):
        conv(w_in_ws, p, 0, 512, w_in, 0, 16, 512 * p)
    conv_finish(w_in_ws)
    for p in (2, 3, 4, 5):
        w_in_ws.ptok[p] = w_in_ws.tok
    w_in_ws.cv = DmaSem(kb, "cvf_in_rest")
    for p in (0, 1, 6, 7, 8, 9, 10, 11, 12, 13, 14, 15):
        conv(w_in_ws, p, 0, 512, w_in, 0, 16, 512 * p)
    conv_finish(w_in_ws)
    for p in range(4):
        conv(W["ap"], p, 0, 512, w_ap, 0, 8, 512 * p)
    conv_finish(W["ap"])
    for p in range(4):
        conv(W["pp"], p, 0, 512, w_pp, 0, 8, 512 * p)
    conv_finish(W["pp"])
    for g in range(4):
        kb.pool.e.dma_start(out=pg_scr[:, 2 * g:2 * g + 2, :],
                            in_=w_pg[g].rearrange("(k p) c -> p k c", p=128)).then_inc(pg_cv.sem, 16)
        pg_cv.cnt += 16
    pg_tok = Tok(pg_cv.sem, pg_cv.cnt, pg_cv.key)
    kb.latest[pg_cv.key] = pg_tok
    for p in range(4):
        conv(W["o"], p, 0, 512, w_o, 0, 16, 512 * p)
    conv_finish(W["o"])
    conv_ffn(1)

    banks = [kb.stack.enter_context(nc.psum_tensor(f"bank{i}", [128, 512], F32)) for i in range(8)]
    bbuf = [Buf(f"bank{i}") for i in range(8)]

    cst = kb.stack
    ident = kb.sbuf(cst, "ident", [128, NH + 1, 128], BF16)
    onesb = kb.sbuf(cst, "onesb", [128, 128], BF16)
    mhalf = kb.sbuf(cst, "mhalf", [128, 4], F32)
    c_sem = DmaSem(kb, "cload")
    cb = ConstBatch(kb, c_sem)
    c_buf = cb.load(kb.act, ident[:], sident_in[:, :, :])
    m_buf = Buf("memsets")
    kb.op(kb.dve, lambda e: e.memset(onesb[:], 1.0), writes=[m_buf])
    kb.op(kb.dve, lambda e: e.memset(mhalf[:], -0.5), writes=[m_buf])
    kb.op(kb.dve, lambda e: e.memset(mhalf[:, 1:2], EPS), writes=[m_buf])

    def load_gain(st, name):
        g = kb.sbuf(st, "g_" + name, [128, D], F32)
        gb = cb.load(kb.act, g[:], norms[name][0:1, :].broadcast_to([128, D]))
        return g, gb

    def rmsnorm_to_uT(xh, xhb, g, gb, uT, uTb, tmp):
        junk, junkb, ss, ssb, rstd, rstdb, xn, xnb = tmp

        def stage_a(ts):
            b = ts % 2
            kb.op(kb.act, lambda e: e.activation(out=junk[:], in_=xh[:, ts, :], func=AF.Square,
                                                 accum_out=ss[:, ts:ts + 1]),
                  reads=[xhb[ts]], writes=[junkb, ssb[ts]])
            kb.op(kb.act, lambda e: e.activation(out=ss[:, ts:ts + 1], in_=ss[:, ts:ts + 1], func=AF.Sqrt,
                                                 scale=1.0 / D, bias=mhalf[:, 1:2]),
                  reads=[ssb[ts], m_buf], writes=[ssb[ts]])
            kb.op(kb.dve, lambda e: e.reciprocal(out=rstd[:, ts:ts + 1], in_=ss[:, ts:ts + 1]),
                  reads=[ssb[ts]], writes=[rstdb[ts]])
            kb.op(kb.dve, lambda e: e.scalar_tensor_tensor(out=xn[b][:], in0=xh[:, ts, :], scalar=rstd[:, ts:ts + 1],
                                                           in1=g[:], op0=ALU.mult, op1=ALU.mult),
                  reads=[xhb[ts], rstdb[ts], gb], writes=[xnb[b]])

        def stage_b(ts):
            b = ts % 2
            for half in range(2):
                bk = 4 + 2 * b + half
                pv = banks[bk].bitcast(BF16)
                mms = []
                for k in range(8):
                    dc = half * 8 + k
                    mms.append((pv[:, k * 128:(k + 1) * 128], xn[b][:, dc * 128:(dc + 1) * 128], ident[:, 0, :]))
                kb.mm_group(mms, reads=[xnb[b], c_buf], writes=[bbuf[bk]], transpose=True)
                src = pv[:, :].rearrange("p (k t) -> p k t", t=128)
                dst = uT[:, half * 8:half * 8 + 8, ts * 128:(ts + 1) * 128]
                ub = uTb[ts * 2 + half]
                if half == 0:
                    kb.op(kb.act, lambda e: e.activation(out=dst, in_=src, func=AF.Copy), reads=[bbuf[bk]], writes=[ub])
                else:
                    kb.op(kb.dve, lambda e: e.tensor_copy(dst, src), reads=[bbuf[bk]], writes=[ub])

        stage_a(0)
        stage_a(1)
        stage_b(0)
        stage_a(2)
        stage_b(1)
        stage_a(3)
        stage_b(2)
        stage_b(3)

    def ffn(stream, wg, wd, uT, uTb, actT, actTb, xh, xhb, sg, sgb, after_dp=None):
        for p in range(22):
            wv, wb = stream.get(wg, p)
            for c in range(2):
                j = 2 * p + c
                pb = (j % 2) * 2
                bg, bu = pb, pb + 1
                kb.mm_group([(banks[bg][:], wv[:, k, c * 128:(c + 1) * 128], uT[:, k, :], k == 0, k == 15) for k in range(16)],
                            reads=[wb] + uTb, writes=[bbuf[bg]])
                kb.mm_group([(banks[bu][:], wv[:, k, 256 + c * 128:256 + (c + 1) * 128], uT[:, k, :], k == 0, k == 15) for k in range(16)],
                            reads=[wb] + uTb, writes=[bbuf[bu]])
                s = j % 2
                kb.op(kb.act, lambda e: e.activation(out=sg[s][:], in_=banks[bg][:], func=AF.Silu),
                      reads=[bbuf[bg]], writes=[sgb[s]])
                kb.op(kb.dve, lambda e: e.tensor_tensor(out=actT[:, j, :], in0=sg[s][:], in1=banks[bu][:], op=ALU.mult),
                      reads=[sgb[s], bbuf[bu]], writes=[actTb[j]])
            stream.done()
        for dp in range(4):
            for part, (j0, kc) in enumerate(((0, 16), (16, 16), (32, 12))):
                wv, wb = stream.get(wd, dp * 3 + part)
                for ts in range(4):
                    bk = 4 + ts
                    mms = [(banks[bk][:], actT[:, j0 + k, ts * 128:(ts + 1) * 128], wv[:, k, :],
                            part == 0 and k == 0, part == 2 and k == kc - 1) for k in range(kc)]
                    kb.mm_group(mms, reads=[wb] + [actTb[j0 + k] for k in range(kc)],
                                writes=[bbuf[bk]] if part == 0 else [], )
                    if part == 2:
                        pass
                stream.done()
            for ts in range(4):
                bk = 4 + ts
                bbuf[bk].w = kb.latest[kb.pe.key]
                dst = xh[:, ts, dp * 512:(dp + 1) * 512]
                kb.op(kb.dve, lambda e: e.scalar_tensor_tensor(out=dst, in0=banks[bk][:], scalar=0.5, in1=dst,
                                                               op0=ALU.mult, op1=ALU.add),
                      reads=[bbuf[bk]], writes=[xhb[ts]])
            if after_dp is not None:
                after_dp(dp)

    with ExitStack() as st:
        seq = []
        for t in range(nt):
            seq += [(W["gu0"], p) for p in range(22)] + [(W["dn0"], p) for p in range(12)]
            seq += [(W["in"], p) for p in (2, 3, 4, 5)]
        stream = WStream(kb, st, seq, "a", nslot=4)
        xh = kb.sbuf(st, "xh", [128, 4, D], F32)
        xhb = [Buf(f"xh{ts}") for ts in range(4)]
        xsem = [DmaSem(kb, f"p1x{ts}") for ts in range(4)]
        g1, g1b = load_gain(st, "ffn1_norm")
        g2, g2b = load_gain(st, "mix_norm")
        cb.finish()
        uT = kb.sbuf(st, "uT", [128, 16, T], BF16)
        uTb = [Buf(f"uT{i}") for i in range(8)]
        actT = kb.sbuf(st, "actT", [128, NJ, T], BF16)
        actTb = [Buf(f"actT{j}") for j in range(NJ)]
        sg = [kb.sbuf(st, f"sg{i}", [128, T], F32) for i in range(2)]
        sgb = [Buf(f"sg{i}") for i in range(2)]
        junk = kb.sbuf(st, "junk", [128, D], BF16)
        ss = kb.sbuf(st, "ss", [128, 4], F32)
        rstd = kb.sbuf(st, "rstd", [128, 4], F32)
        xn = [kb.sbuf(st, f"xn{i}", [128, D], BF16) for i in range(2)]
        tmp = (junk, Buf("junk"), ss, [Buf(f"ss{i}") for i in range(4)], rstd, [Buf(f"rstd{i}") for i in range(4)],
               xn, [Buf("xn0"), Buf("xn1")])
        kst = [kb.sbuf(st, f"kst{i}", [128, T], BF16) for i in range(2)]
        kstb = [Buf(f"kst{i}") for i in range(2)]
        ksem = [DmaSem(kb, f"kst{i}") for i in range(2)]
        vst = kb.sbuf(st, "vst", [128, NH, 4, 128], BF16)
        vstb = [Buf(f"vst{i}") for i in range(8)]
        vsem = DmaSem(kb, "vst")

        def load_x(t):
            for ts in range(4):
                kb.dma(kb.act, xsem[ts], xh[:, ts, :], x_in[t * T + ts * 128:t * T + (ts + 1) * 128, :], writes=[xhb[ts]])

        load_x(0)
        for t in range(nt):
            r0 = t * T
            rmsnorm_to_uT(xh, xhb, g1, g1b, uT, uTb, tmp)
            ffn(stream, W["gu0"], W["dn0"], uT, uTb, actT, actTb, xh, xhb, sg, sgb)
            for ts in range(4):
                kb.dma(kb.act, xsem[ts], h_scr[r0 + ts * 128:r0 + (ts + 1) * 128, :], xh[:, ts, :], reads=[xhb[ts]])
            rmsnorm_to_uT(xh, xhb, g2, g2b, uT, uTb, tmp)
            if t + 1 < nt:
                load_x(t + 1)
            for pi, p in enumerate((2, 3)):
                wv, wb = stream.get(W["in"], p)
                for c in range(4):
                    h = 4 * pi + c
                    bk = h % 4
                    kb.mm_group([(banks[bk][:], wv[:, k, c * 128:(c + 1) * 128], uT[:, k, :], k == 0, k == 15) for k in range(16)],
                                reads=[wb] + uTb, writes=[bbuf[bk]])
                    s = h % 2
                    if s == 0:
                        kb.op(kb.act, lambda e: e.activation(out=kst[s][:], in_=banks[bk][:], func=AF.Copy),
                              reads=[bbuf[bk]], writes=[kstb[s]])
                    else:
                        kb.op(kb.dve, lambda e: e.tensor_copy(kst[s][:], banks[bk][:]), reads=[bbuf[bk]], writes=[kstb[s]])
                    kb.dma(kb.act, ksem[s], kt_scr[h, :, r0:r0 + T], kst[s][:], reads=[kstb[s]])
                stream.done()
            for pi, p in enumerate((4, 5)):
                wv, wb = stream.get(W["in"], p)
                for ts in range(4):
                    bk = 4 + ts
                    kb.mm_group([(banks[bk][:], uT[:, k, ts * 128:(ts + 1) * 128], wv[:, k, :], k == 0, k == 15) for k in range(16)],
                                reads=[wb] + uTb, writes=[bbuf[bk]])
                    dst = vst[:, 4 * pi:4 * pi + 4, ts, :]
                    src = banks[bk][:, :].rearrange("p (h d) -> p h d", d=128)
                    if ts % 2 == 0:
                        kb.op(kb.act, lambda e: e.activation(out=dst, in_=src, func=AF.Copy), reads=[bbuf[bk]], writes=[vstb[pi * 4 + ts]])
                    else:
                        kb.op(kb.dve, lambda e: e.tensor_copy(dst, src), reads=[bbuf[bk]], writes=[vstb[pi * 4 + ts]])
                stream.done()
            kb.dma(kb.act, vsem, v_scr[:, :, 4 * t:4 * t + 4, :].rearrange("h p s d -> p h s d"), vst[:],
                   reads=vstb)
        kb.barrier()

    if dbg == "p1":
        dsem = DmaSem(kb, "dbg")
        kb.dma(kb.sp, dsem, y_out[:, :], h_scr[:, :])
        kb.sp.wait(kb.latest["dbg"])
        return kb

    nA = kb.nA
    with ExitStack() as st:
        seq = []
        for t in range(nt):
            seq += [(W["in"], p) for p in (6, 0, 7, 1)]
            for mg in range(4):
                seq += [(W["in"], 12 + mg), (W["pp"], mg), (W["in"], 8 + mg), (W["ap"], mg)]
            seq += [(W["o"], p) for p in range(4)]
        stream = WStream(kb, st, seq, "b")
        cb2 = ConstBatch(kb, c_sem)
        gm, gmb = (lambda g: (g, cb2.load(kb.act, g[:], norms["mix_norm"][0:1, :].broadcast_to([128, D]))))(
            kb.sbuf(st, "g_mix2", [128, D], F32))
        hs = [kb.sbuf(st, f"hs{i}", [128, D], F32) for i in range(2)]
        hsb = [Buf(f"hs{i}") for i in range(2)]
        hsem = [DmaSem(kb, f"hs{i}") for i in range(2)]
        xn = [kb.sbuf(st, f"xnb{i}", [128, D], BF16) for i in range(2)]
        xnb = [Buf("xnb0"), Buf("xnb1")]
        ss = kb.sbuf(st, "ss2", [128, 8], F32)
        ssb = [Buf(f"ss2{i}") for i in range(5)]
        rstd = kb.sbuf(st, "rstd2", [128, 8], F32)
        rstdb = [Buf(f"rstd2{i}") for i in range(5)]
        uT = kb.sbuf(st, "uT2", [128, 16, T], BF16)
        uTb = [Buf(f"uT2{i}") for i in range(8)]
        uTh = kb.sbuf(st, "uTh", [128, 16, 16], BF16)
        uThb = Buf("uTh")
        qT = kb.sbuf(st, "qT", [128, NH, T], BF16)
        qTb = [Buf(f"qT{h}") for h in range(NH)]
        qaugb = [Buf(f"qaug{h}") for h in range(NH)]
        xaugb = [Buf(f"xaug{h}") for h in range(NH)]
        qasem = DmaSem(kb, "qaug")
        nkA = nA * 4
        nkB = (nt - nA) * 4
        kb.op(kb.pool, lambda e: e.memset(qT[64:128, :, :], 0.0), writes=qTb)
        dbias = kb.sbuf(st, "dbias", [128, 896], F32)
        dbb = cb2.load(kb.act, dbias[:], dbias_in[:, :])
        HK = 8
        kx = [kb.sbuf(st, f"kx{i}", [128, HK * 128], BF16) for i in range(2)]
        ky = [kb.sbuf(st, f"ky{i}", [128, HK * 128], BF16) for i in range(2)]
        vbuf = [kb.sbuf(st, f"vbuf{i}", [128, HK, 128], BF16) for i in range(2)]
        kvb = [Buf("kv0"), Buf("kv1")]
        kvsem = [DmaSem(kb, "kv0"), DmaSem(kb, "kv1")]
        for i_ in range(2):
            kb.op(kb.pool, lambda e: e.memset(kx[i_][:], 0.0), writes=[kvb[i_]])
            kb.op(kb.pool, lambda e: e.memset(ky[i_][:], 0.0), writes=[kvb[i_]])
        rl0 = kb.sbuf(st, "rl0", [128, T], F32)
        rl0b = Buf("rl0")
        Et = [[kb.sbuf(st, f"E{i}{c}", [128, T], BF16) for c in range(2)] for i in range(2)]
        Eb = [[Buf(f"E{i}{c}") for c in range(2)] for i in range(2)]
        r0 = kb.sbuf(st, "r0", [128, T], F32)
        r1 = kb.sbuf(st, "r1", [128, T], F32)
        r0b, r1b = Buf("r0"), Buf("r1")
        sq = kb.sbuf(st, "sq", [128, T], BF16)
        sqb = Buf("sq")
        acc2 = [[kb.sbuf(st, f"acc{p}{c}", [128, T], F32) for c in range(2)] for p in range(2)]
        accb2 = [[Buf(f"acc{p}{c}") for c in range(2)] for p in range(2)]
        hblk_t = [r0, r1, rl0, acc2[0][0], acc2[0][1], acc2[1][0], acc2[1][1]]
        hblk_b = [r0b, r1b, rl0b, accb2[0][0], accb2[0][1], accb2[1][0], accb2[1][1]]
        hbsem = [DmaSem(kb, f"hblk{i}") for i in range(7)]
        ones32 = kb.sbuf(st, "ones32", [128, 128], F32)
        kb.op(kb.pool, lambda e: e.memset(ones32[:], 1.0), writes=[m_buf])
        aT = kb.sbuf(st, "aT", [128, NH, T], BF16)
        aTb = [Buf(f"aT{h}") for h in range(NH)]
        pin = kb.sbuf(st, "pin", [128, 2, T + 16], F32)
        TA = kb.sbuf(st, "TA", [128, 2, T + 16], F32)
        TB = kb.sbuf(st, "TB", [128, 2, T + 16], F32)
        pinb, TAb, TBb = Buf("pin"), Buf("TA"), Buf("TB")
        pooledT = [kb.sbuf(st, f"pooledT{i}", [128, 4 * T], BF16) for i in range(2)]
        pldb = [Buf("pld0"), Buf("pld1")]
        ppT = kb.sbuf(st, "ppT", [128, 8, T], BF16)
        ppTb = [Buf(f"ppT{i}") for i in range(8)]
        icnt = [kb.sbuf(st, f"icnt{i}", [128, T], F32) for i in range(2)]
        icb = [Buf("icnt0"), Buf("icnt1")]
        icsem = [DmaSem(kb, "ic0"), DmaSem(kb, "ic1")]
        pgw = kb.sbuf(st, "pgw", [128, 8, 256], BF16)
        kb.act.wait(pg_tok)
        pgb = cb2.load(kb.act, pgw[:], pg_scr[:, :, :])
        mg_t = kb.sbuf(st, "merged", [128, 16, T], BF16)
        mgb = [Buf(f"mg{i}") for i in range(16)]
        lamv = kb.sbuf(st, "lamv", [128, 256], F32)
        lamb_ = cb2.load(kb.act, lamv[:], lam_in[0:1, :].broadcast_to([128, 256]))
        gsub = kb.sbuf(st, "gsub", [128, 1], F32)
        gsubb = cb2.load(kb.act, gsub[:], subln_in[:, :])
        psc = kb.sbuf(st, "psc", [128, 8], F32)
        pscb = cb2.load(kb.act, psc[:], pscale_in[:, :])
        hmask = kb.sbuf(st, "hmask", [128, 2 * nt], F32)
        hmb = cb2.load(kb.act, hmask[:], hmask_in[:, :])
        cb2.finish()
        small = kb.sbuf(st, "small", [128, 8], F32)
        smb = Buf("small")
        ljunk = kb.sbuf(st, "ljunk", [128, 64], F32)

        kb.op(kb.dve, lambda e: e.scalar_tensor_tensor(out=ljunk[:], in0=lamv[:, 0:64], scalar=1.0, in1=lamv[:, 64:128],
                                                       op0=ALU.mult, op1=ALU.mult, accum_out=small[:, 0:1]),
              reads=[lamb_], writes=[smb])
        kb.op(kb.dve, lambda e: e.scalar_tensor_tensor(out=ljunk[:], in0=lamv[:, 128:192], scalar=1.0, in1=lamv[:, 192:256],
                                                       op0=ALU.mult, op1=ALU.mult, accum_out=small[:, 1:2]),
              reads=[lamb_, smb], writes=[smb])
        kb.op(kb.act, lambda e: e.activation(out=small[:, 2:4], in_=small[:, 0:2], func=AF.Exp), reads=[smb], writes=[smb])
        kb.op(kb.dve, lambda e: e.tensor_tensor(out=small[:, 4:5], in0=small[:, 3:4], in1=small[:, 2:3], op=ALU.subtract),
              reads=[smb], writes=[smb])
        kb.op(kb.dve, lambda e: e.tensor_scalar(out=small[:, 5:6], in0=small[:, 4:5], scalar1=-LAMBDA_INIT, scalar2=None,
                                                op0=ALU.add), reads=[smb], writes=[smb])
        kb.op(kb.dve, lambda e: e.tensor_scalar(out=small[:, 6:7], in0=gsub[:, 0:1], scalar1=1.0 - LAMBDA_INIT, scalar2=None,
                                                op0=ALU.mult), reads=[smb, gsubb], writes=[smb])
        neglam = small[:, 5:6]
        gsc = small[:, 6:7]

        def group_of(t):
            return (0, nkA, 0) if t < nA else (nA * 4, nkB, 1)

        def load_kv(h, kt0, n, slot, qt):
            k0g_, _, _ = group_of(qt)
            c0, c1 = kt0 * 128, (kt0 + n) * 128
            kb.dma(kb.sp, kvsem[slot], kx[slot][0:64, 0:n * 128], kt_scr[h, 0:64, c0:c1], writes=[kvb[slot]])
            kb.dma(kb.sp, kvsem[slot], ky[slot][64:128, 0:n * 128], kt_scr[h, 64:128, c0:c1], writes=[])
            kb.dma(kb.sp, kvsem[slot], vbuf[slot][:, 0:n, :], v_scr[h, :, kt0:kt0 + n, :], writes=[])
            qlo = qt * 4
            segs = []
            for (a_, b_, v_) in ((kt0, min(kt0 + n, qlo), 0), (max(kt0, qlo), min(kt0 + n, qlo + 4), 2),
                                 (max(kt0, qlo + 4), kt0 + n, 1)):
                if b_ > a_:
                    segs.append((a_, b_, v_))
            for (a_, b_, v_) in segs:
                kb.dma(kb.sp, kvsem[slot], kx[slot][64:72, (a_ - kt0) * 128:(b_ - kt0) * 128],
                       ka_in[v_, :, a_ * 128:b_ * 128], writes=[])
                kb.dma(kb.sp, kvsem[slot], ky[slot][32:40, (a_ - kt0) * 128:(b_ - kt0) * 128],
                       ka_in[v_, :, a_ * 128:b_ * 128], writes=[])
            kvb[slot].w = kb.latest[kvsem[slot].key]

        kvcount = [0]

        def kv_plan(t):
            k0, nk, _ = group_of(t)
            plan = []
            for h in range(NH):
                for c0 in range(0, nk, HK):
                    plan.append((h, k0 + c0, min(HK, nk - c0)))
            return plan

        for t in range(nt):
            rbase = t * T
            k0g, nkg, gid = group_of(t)
            plan = kv_plan(t)
            pl_i = 0
            slot_of = {}
            for _ in range(min(2, len(plan))):
                slot = kvcount[0] % 2
                load_kv(*plan[pl_i], slot, t)
                slot_of[pl_i] = slot
                kvcount[0] += 1
                pl_i += 1

            def load_slice(ts, b, rb):
                if ts < 4:
                    kb.dma(kb.act, hsem[b], hs[b][:, :], h_scr[rb + ts * 128:rb + (ts + 1) * 128, :], writes=[hsb[b]])
                else:
                    lo = max(rb - 8, 0)
                    hi = min(rb + T, ntok - 8)
                    kb.dma(kb.act, hsem[b], hs[b][0:8, :], h_scr[lo:lo + 8, :], writes=[hsb[b]])
                    kb.dma(kb.act, hsem[b], hs[b][8:16, :], h_scr[hi:hi + 8, :], writes=[])
                    hsb[b].w = kb.latest[hsem[b].key]

            def p2_stage_a(ts, rb):
                b = ts % 2
                np_ = 128 if ts < 4 else 16
                kb.op(kb.act, lambda e: e.activation(out=xn[b][0:np_, :], in_=hs[b][0:np_, :], func=AF.Square,
                                                     accum_out=ss[0:np_, ts:ts + 1]),
                      reads=[hsb[b]], writes=[xnb[b], ssb[ts]])
                kb.op(kb.act, lambda e: e.activation(out=ss[0:np_, ts:ts + 1], in_=ss[0:np_, ts:ts + 1], func=AF.Sqrt,
                                                     scale=1.0 / D, bias=mhalf[0:np_, 1:2]),
                      reads=[ssb[ts], m_buf], writes=[ssb[ts]])
                kb.op(kb.dve, lambda e: e.reciprocal(out=rstd[0:np_, ts:ts + 1], in_=ss[0:np_, ts:ts + 1]),
                      reads=[ssb[ts]], writes=[rstdb[ts]])
                kb.op(kb.dve, lambda e: e.scalar_tensor_tensor(out=xn[b][0:np_, :], in0=hs[b][0:np_, :],
                                                               scalar=rstd[0:np_, ts:ts + 1], in1=gm[0:np_, :],
                                                               op0=ALU.mult, op1=ALU.mult),
                      reads=[hsb[b], rstdb[ts], gmb], writes=[xnb[b]])
                if ts + 2 < 5:
                    load_slice(ts + 2, b, rb)

            def p2_stage_b(ts):
                b = ts % 2
                np_ = 128 if ts < 4 else 16
                for half in range(2):
                    bk = 4 + 2 * b + half
                    pv = banks[bk].bitcast(BF16)
                    mms = []
                    for k in range(8):
                        dc = half * 8 + k
                        mms.append((pv[:, k * 128:k * 128 + np_], xn[b][0:np_, dc * 128:(dc + 1) * 128],
                                    ident[0:np_, 0, 0:np_]))
                    kb.mm_group(mms, reads=[xnb[b], c_buf], writes=[bbuf[bk]], transpose=True)
                    src = pv[:, :].rearrange("p (k t) -> p k t", t=128)[:, :, 0:np_]
                    if ts < 4:
                        dst = uT[:, half * 8:half * 8 + 8, ts * 128:(ts + 1) * 128]
                        wb_ = uTb[ts * 2 + half]
                    else:
                        dst = uTh[:, half * 8:half * 8 + 8, :]
                        wb_ = uThb
                    if half == 0:
                        kb.op(kb.act, lambda e: e.activation(out=dst, in_=src, func=AF.Copy), reads=[bbuf[bk]], writes=[wb_])
                    else:
                        kb.op(kb.dve, lambda e: e.tensor_copy(dst, src), reads=[bbuf[bk]], writes=[wb_])

            if t == 0:
                load_slice(0, 0, rbase)
                load_slice(1, 1, rbase)
                p2_stage_a(0, rbase)
                for ts in range(5):
                    if ts + 1 < 5:
                        p2_stage_a(ts + 1, rbase)
                    p2_stage_b(ts)

            for i_ in range(2):
                kb.op(kb.pool, lambda e: e.memset(xn[i_][0:64, :], 0.0), writes=[xnb[i_]])
            for h in range(NH):
                kb.deps(kb.act, [xnb[h // 4], qTb[h]], [qaugb[h], xaugb[h]])
                kb.act.e.dma_start(out=qT[64:72, h, :], in_=qa_in[h, :, rbase:rbase + T]).then_inc(qasem.sem, 16)
                kb.act.e.dma_start(out=xn[h // 4][32:40, (h % 4) * T:(h % 4 + 1) * T],
                                   in_=qa_in[h, :, rbase:rbase + T]).then_inc(qasem.sem, 16)
                qasem.cnt += 32
            qtok = Tok(qasem.sem, qasem.cnt, qasem.key)
            kb.latest[qasem.key] = qtok
            for h in range(NH):
                kb.commit(qtok, [xnb[h // 4]], [qaugb[h], xaugb[h]])
            for pi in range(2):
                p = 6 + pi
                wv, wb = stream.get(W["in"], p)
                for gl in range(2):
                    g = 2 * pi + gl
                    w = 2 << g
                    ib = g % 2
                    kb.dma(kb.act, icsem[ib], icnt[ib][:], icnt_in[g:g + 1, rbase:rbase + T].broadcast_to([128, T]),
                           writes=[icb[ib]])
                    for cc in range(2):
                        c = 2 * gl + cc
                        bk = c % 4
                        kb.mm_group([(banks[bk][:], wv[:, k, c * 128:(c + 1) * 128], uT[:, k, :], k == 0, k == 15) for k in range(16)],
                                    reads=[wb] + uTb, writes=[bbuf[bk]])
                        kb.op(kb.act, lambda e: e.activation(out=pin[:, cc, 8:8 + T], in_=banks[bk][:], func=AF.Copy),
                              reads=[bbuf[bk]], writes=[pinb])
                        kb.mm_group([(banks[bk][:, 0:16], wv[:, k, c * 128:(c + 1) * 128], uTh[:, k, :], k == 0, k == 15) for k in range(16)],
                                    reads=[wb, uThb], writes=[bbuf[bk]])
                        kb.op(kb.dve, lambda e: e.tensor_scalar(out=pin[:, cc, 0:8], in0=banks[bk][:, 0:8],
                                                                scalar1=hmask[:, 2 * t:2 * t + 1], scalar2=None, op0=ALU.mult),
                              reads=[bbuf[bk], hmb], writes=[pinb])
                        kb.op(kb.dve, lambda e: e.tensor_scalar(out=pin[:, cc, 8 + T:16 + T], in0=banks[bk][:, 8:16],
                                                                scalar1=hmask[:, 2 * t + 1:2 * t + 2], scalar2=None, op0=ALU.mult),
                              reads=[bbuf[bk], hmb], writes=[pinb])
                    N = T + 16
                    kb.op(kb.pool, lambda e: e.tensor_tensor(out=TA[:, :, 1:N], in0=pin[:, :, 0:N - 1], in1=pin[:, :, 1:N], op=ALU.add),
                          reads=[pinb], writes=[TAb])
                    cur, curb, oth, othb = TA, TAb, TB, TBb
                    sh = 1
                    lo, hi = 1, N
                    for step in range(g):
                        nlo, nhi = lo + sh, hi - sh
                        kb.op(kb.pool, lambda e: e.tensor_tensor(out=oth[:, :, nlo:nhi], in0=cur[:, :, nlo - sh:nhi - sh],
                                                                 in1=cur[:, :, nlo + sh:nhi + sh], op=ALU.add),
                              reads=[curb], writes=[othb])
                        cur, curb, oth, othb = oth, othb, cur, curb
                        lo, hi = nlo, nhi
                        sh *= 2
                    assert lo <= 8 and hi >= 8 + T, (g, lo, hi)
                    pooled = pooledT[pi]
                    for cc in range(2):
                        c = 2 * gl + cc
                        kb.op(kb.pool, lambda e: e.tensor_tensor(out=oth[:, cc, 8:8 + T], in0=cur[:, cc, 8:8 + T], in1=icnt[ib][:],
                                                                 op=ALU.mult),
                              reads=[curb, icb[ib]], writes=[othb])
                        kb.op(kb.pool, lambda e: e.tensor_tensor(out=pooled[:, c * T:(c + 1) * T], in0=oth[:, cc, 8:8 + T],
                                                                 in1=pin[:, cc, 8:8 + T], op=ALU.subtract),
                              reads=[othb, pinb], writes=[pldb[pi]])
                stream.done()
                p = pi
                wv, wb = stream.get(W["in"], p)
                for c in range(4):
                    h = 4 * pi + c
                    bk = h % 4
                    kb.mm_group([(banks[bk][:], wv[:, k, c * 128:(c + 1) * 128], uT[:, k, :], k == 0, k == 15) for k in range(16)],
                                reads=[wb] + uTb, writes=[bbuf[bk]])
                    kb.op(kb.act, lambda e: e.activation(out=qT[0:64, h, :], in_=banks[bk][0:64, :], func=AF.Copy),
                          reads=[bbuf[bk]], writes=[qTb[h]])
                    kb.op(kb.dve, lambda e: e.tensor_copy(xn[pi][64:128, c * T:(c + 1) * T], banks[bk][64:128, :]),
                          reads=[bbuf[bk]], writes=[xnb[pi]])
                stream.done()

                pooled = pooledT[pi]
                for gl in range(2):
                    g = 2 * pi + gl
                    for oc in range(2):
                        bk = (2 * g + oc) % 4
                        kb.mm_group([(banks[bk][:], pgw[:, 2 * g + k, oc * 128:(oc + 1) * 128],
                                      pooled[:, (2 * gl + k) * T:(2 * gl + k + 1) * T], k == 0, k == 1) for k in range(2)],
                                    reads=[pgb, pldb[pi]], writes=[bbuf[bk]])
                        ci = 2 * g + oc
                        kb.op(kb.dve, lambda e: e.tensor_scalar(out=ppT[:, ci, :], in0=banks[bk][:], scalar1=psc[:, ci:ci + 1],
                                                                scalar2=None, op0=ALU.mult),
                              reads=[bbuf[bk], pscb], writes=[ppTb[ci]])

            nparts = (nkg + HK - 1) // HK
            blocks = []
            for h in range(NH):
                for part in range(nparts):
                    ph, kt0, n = plan[h * nparts + part]
                    assert ph == h
                    for i in range(n):
                        blocks.append((h, h * nparts + part, i, kt0 + i, i == n - 1,
                                       part == 0 and i == 0, part == nparts - 1 and i == n - 1))
            NB = len(blocks)

            usedP = {}

            def emit_qk(g):
                h, pix, i, kt, _, fh, _ = blocks[g]
                slot = slot_of[pix]
                kpos = (kt - k0g) * 128
                qpos = rbase - k0g * 128
                diag = not (kpos + 128 <= qpos or kpos >= qpos + T)
                di = (kpos - qpos) // 128
                g_ = g % 2
                ops = ((kx[slot], qT[:, h, :], [kvb[slot], qTb[h], qaugb[h]]),
                       (ky[slot], xn[h // 4][:, (h % 4) * T:(h % 4 + 1) * T], [kvb[slot], xnb[h // 4], xaugb[h]]))
                for c in range(2):
                    bk = g_ * 2 + c
                    kt_, qz_, rd = ops[c]
                    mms = [(banks[bk][:], kt_[:, i * 128:(i + 1) * 128], qz_, True, True)]
                    kb.mm_group(mms, reads=rd, writes=[bbuf[bk]])
                    if diag:
                        kb.op(kb.dve, lambda e: e.scalar_tensor_tensor(out=banks[bk][:], in0=dbias[:, 384 - 128 * di:896 - 128 * di],
                                                                       scalar=-8.0 * SLOPES[h], in1=banks[bk][:],
                                                                       op0=ALU.mult, op1=ALU.add),
                              reads=[dbb, bbuf[bk]], writes=[bbuf[bk]])
                    kb.op(kb.act, lambda e: e.activation(out=Et[g_][c][:], in_=banks[bk][:], func=AF.Exp, scale=QK_SCALE),
                          reads=[bbuf[bk]], writes=[Eb[g_][c]])

            def emit_pv(g):
                h, pix, i, kt, _, fh, lh = blocks[g]
                slot = slot_of[pix]
                g_ = g % 2
                for c in range(2):
                    kb.mm_group([(banks[4 + c][:], vbuf[slot][:, i, :], Et[g_][c][:], fh, lh)],
                                reads=[kvb[slot], Eb[g_][c]], writes=[bbuf[4 + c]] if fh else [])
                kb.mm_group([(banks[6][:], onesb[:], Et[g_][0][:], fh, lh)],
                            reads=[m_buf, Eb[g_][0]], writes=[bbuf[6]] if fh else [])
                j = hblk[g] % 2
                acc, accb = acc2[h % 2], accb2[h % 2]
                eng = kb.dve if j == 0 else kb.pool
                if hblk[g] < 2:
                    kb.op(eng, lambda e: e.tensor_copy(acc[j][:], Et[g_][1][:]), reads=[Eb[g_][1]], writes=[accb[j]])
                    if j == 1:
                        usedP[h] = True
                else:
                    kb.op(eng, lambda e: e.tensor_tensor(out=acc[j][:], in0=acc[j][:], in1=Et[g_][1][:], op=ALU.add),
                          reads=[Eb[g_][1], accb[j]], writes=[accb[j]])

            def evac_head(h):
                for bk in (4, 5, 6):
                    bbuf[bk].w = kb.latest[kb.pe.key]
                kb.op(kb.act, lambda e: e.activation(out=r0[:], in_=banks[4][:], func=AF.Copy), reads=[bbuf[4]], writes=[r0b])
                kb.op(kb.dve, lambda e: e.tensor_copy(r1[:], banks[5][:]), reads=[bbuf[5]], writes=[r1b])
                kb.op(kb.dve, lambda e: e.tensor_copy(rl0[:], banks[6][:]), reads=[bbuf[6]], writes=[rl0b])

            def evac_steps(h):
                st_ = []
                acc, accb = acc2[h % 2], accb2[h % 2]
                up = usedP.get(h, False)

                def l1_():
                    mm1 = [(banks[7][:], ones32[:], acc[0][:], True, not up)]
                    if up:
                        mm1.append((banks[7][:], ones32[:], acc[1][:], False, True))
                    kb.mm_group(mm1, reads=[m_buf, accb[0], accb[1]], writes=[bbuf[7]])
                st_.append(l1_)

                def rq_(tile_ap, buf, q):
                    return lambda: kb.op(kb.dve, lambda e: e.reciprocal(out=tile_ap[:, q * 128:(q + 1) * 128],
                                                                        in_=tile_ap[:, q * 128:(q + 1) * 128]),
                                         reads=[buf], writes=[buf])
                for q in range(4):
                    st_.append(("heavy", rq_(rl0, rl0b, q)))
                st_.append(lambda: kb.op(kb.dve, lambda e: e.tensor_tensor(out=r0[:], in0=r0[:], in1=rl0[:], op=ALU.mult),
                                         reads=[rl0b, r0b], writes=[r0b]))
                for q in range(4):
                    st_.append(("heavy", rq_(banks[7], bbuf[7], q)))
                st_.append(lambda: kb.op(kb.dve, lambda e: e.tensor_tensor(out=r1[:], in0=r1[:], in1=banks[7][:], op=ALU.mult),
                                         reads=[bbuf[7], r1b], writes=[r1b]))
                st_.append(lambda: kb.op(kb.dve, lambda e: e.scalar_tensor_tensor(out=r0[:], in0=r1[:], scalar=neglam, in1=r0[:],
                                                                               op0=ALU.mult, op1=ALU.add),
                                         reads=[r1b, r0b, smb], writes=[r0b]))
                st_.append(lambda: kb.op(kb.act, lambda e: e.activation(out=sq[:], in_=r0[:], func=AF.Square), reads=[r0b], writes=[sqb]))
                st_.append(lambda: kb.mm_group([(banks[7][:], onesb[:], sq[:], True, True)], reads=[m_buf, sqb], writes=[bbuf[7]]))
                st_.append(lambda: kb.op(kb.dve, lambda e: e.tensor_scalar(out=r1[:], in0=banks[7][:], scalar1=1.0 / 128, scalar2=EPS,
                                                                        op0=ALU.mult, op1=ALU.add),
                                         reads=[bbuf[7]], writes=[r1b]))
                st_.append(lambda: kb.op(kb.act, lambda e: e.activation(out=r1[:], in_=r1[:], func=AF.Ln), reads=[r1b], writes=[r1b]))
                st_.append(lambda: kb.op(kb.act, lambda e: e.activation(out=r1[:], in_=r1[:], func=AF.Exp, scale=-0.5),
                                         reads=[r1b], writes=[r1b]))
                st_.append(lambda: kb.op(kb.dve, lambda e: e.scalar_tensor_tensor(out=aT[:, h, :], in0=r0[:], scalar=gsc, in1=r1[:],
                                                                               op0=ALU.mult, op1=ALU.mult),
                                         reads=[r0b, r1b, smb], writes=[aTb[h]]))
                return st_

            hblk = []
            cnt_ = {}
            for (h_, *_r) in blocks:
                hblk.append(cnt_.get(h_, 0))
                cnt_[h_] = cnt_.get(h_, 0) + 1
            emit_qk(0)
            steps = []
            for g in range(NB):
                if g + 1 < NB:
                    emit_qk(g + 1)
                emit_pv(g)
                h, pix, i, kt, plast, fh, lh = blocks[g]
                if plast:
                    slot = slot_of[pix]
                    if pl_i < len(plan):
                        load_kv(*plan[pl_i], slot, t)
                        slot_of[pl_i] = slot
                        kvcount[0] += 1
                        pl_i += 1
                def run_step():
                    st1 = steps.pop(0)
                    if isinstance(st1, tuple):
                        st1[1]()
                        return True
                    st1()
                    return False

                if lh:
                    while steps:
                        run_step()
                    evac_head(h)
                    steps = evac_steps(h)
                elif steps:
                    heavy = run_step()
                    if steps and not heavy:
                        run_step()
            while steps:
                st1 = steps.pop(0)
                (st1[1] if isinstance(st1, tuple) else st1)()

            gt = ((acc2[0][0], accb2[0][0]), (acc2[0][1], accb2[0][1]))
            for mg in range(4):
                for pas, (gp_, wsn, src_, srcb_) in enumerate(((12 + mg, "pp", ppT, ppTb), (8 + mg, "ap", aT, aTb))):
                    wg_, wgb = stream.get(W["in"], gp_)
                    stream.nuse += 1
                    wa_, wab = stream.get(W[wsn], mg)
                    stream.nuse -= 1
                    for c in range(4):
                        m = 4 * mg + c
                        b0 = (m % 2) * 2
                        kb.mm_group([(banks[b0][:], wg_[:, k, c * 128:(c + 1) * 128], uT[:, k, :], k == 0, k == 15) for k in range(16)],
                                    reads=[wgb] + uTb, writes=[bbuf[b0]])
                        kb.mm_group([(banks[b0 + 1][:], wa_[:, k, c * 128:(c + 1) * 128], src_[:, k, :], k == 0, k == 7) for k in range(8)],
                                    reads=[wab] + srcb_, writes=[bbuf[b0 + 1]])
                        rr, rrb = gt[m % 2]
                        kb.op(kb.act, lambda e: e.activation(out=rr[:], in_=banks[b0][:], func=AF.Sigmoid), reads=[bbuf[b0]], writes=[rrb])
                        if pas == 0:
                            kb.op(kb.dve, lambda e: e.tensor_tensor(out=mg_t[:, m, :], in0=rr[:], in1=banks[b0 + 1][:], op=ALU.mult),
                                  reads=[rrb, bbuf[b0 + 1]], writes=[mgb[m]])
                        else:
                            kb.op(kb.dve, lambda e: e.tensor_tensor(out=rr[:], in0=rr[:], in1=banks[b0 + 1][:], op=ALU.mult),
                                  reads=[rrb, bbuf[b0 + 1]], writes=[rrb])
                            kb.op(kb.dve, lambda e: e.tensor_tensor(out=mg_t[:, m, :], in0=rr[:], in1=mg_t[:, m, :], op=ALU.add),
                                  reads=[rrb, mgb[m]], writes=[mgb[m]])
                    stream.done()
                    stream.done()

            nxt = (t + 1) * T if t + 1 < nt else None
            if nxt is not None:
                load_slice(0, 0, nxt)
                load_slice(1, 1, nxt)
                p2_stage_a(0, nxt)
            for dp in range(4):
                wv, wb = stream.get(W["o"], dp)
                blks = []
                for ts in range(4):
                    n_ = (dp * 4 + ts) % len(hblk_t)
                    blks.append(n_)
                    kb.dma(kb.pool, hbsem[n_], hblk_t[n_][:], h_scr[rbase + ts * 128:rbase + (ts + 1) * 128, dp * T:(dp + 1) * T],
                           writes=[hblk_b[n_]])
                for ts in range(4):
                    bk = ts
                    n_ = blks[ts]
                    kb.mm_group([(banks[bk][:], mg_t[:, k, ts * 128:(ts + 1) * 128], wv[:, k, :], k == 0, k == 15) for k in range(16)],
                                reads=[wb] + mgb, writes=[bbuf[bk]])
                    kb.op(kb.dve, lambda e: e.tensor_tensor(out=hblk_t[n_][:], in0=banks[bk][:], in1=hblk_t[n_][:], op=ALU.add),
                          reads=[bbuf[bk], hblk_b[n_]], writes=[hblk_b[n_]])
                    kb.dma(kb.pool, hbsem[n_], h2_scr[rbase + ts * 128:rbase + (ts + 1) * 128, dp * T:(dp + 1) * T], hblk_t[n_][:],
                           reads=[hblk_b[n_]])
                stream.done()
                if nxt is not None:
                    p2_stage_a(dp + 1, nxt)
                    p2_stage_b(dp)
            if nxt is not None:
                p2_stage_b(4)
        kb.barrier()

    if dbg == "p2a":
        dsem = DmaSem(kb, "dbg")
        kb.dma(kb.sp, dsem, y_out[:, :], h2_scr[:, :])
        kb.sp.wait(kb.latest["dbg"])
        return kb

    with ExitStack() as st:
        seq = []
        for t in range(nt):
            seq += [(W["gu1"], p) for p in range(22)] + [(W["dn1"], p) for p in range(12)]
        stream = WStream(kb, st, seq, "c")
        xh = kb.sbuf(st, "xh3", [128, 4, D], F32)
        xhb = [Buf(f"xh3{ts}") for ts in range(4)]
        xsem = [DmaSem(kb, f"p3x{ts}") for ts in range(4)]
        cb3 = ConstBatch(kb, c_sem)
        g1 = kb.sbuf(st, "g_ffn2", [128, D], F32)
        g1b = cb3.load(kb.act, g1[:], norms["ffn2_norm"][0:1, :].broadcast_to([128, D]))
        g2 = kb.sbuf(st, "g_fin", [128, D], F32)
        g2b = cb3.load(kb.act, g2[:], norms["final_norm"][0:1, :].broadcast_to([128, D]))
        cb3.finish()
        uT = kb.sbuf(st, "uT3", [128, 16, T], BF16)
        uTb = [Buf(f"uT3{i}") for i in range(8)]
        actT = kb.sbuf(st, "actT3", [128, NJ, T], BF16)
        actTb = [Buf(f"actT3{j}") for j in range(NJ)]
        sg = [kb.sbuf(st, f"sg3{i}", [128, T], F32) for i in range(2)]
        sgb = [Buf(f"sg3{i}") for i in range(2)]
        junk = kb.sbuf(st, "junk3", [128, D], BF16)
        ss = kb.sbuf(st, "ss3", [128, 4], F32)
        rstd = kb.sbuf(st, "rstd3", [128, 4], F32)
        xn = [kb.sbuf(st, f"xn3{i}", [128, D], BF16) for i in range(2)]
        junkb = Buf("junk3")
        ssb = [Buf(f"ss3{i}") for i in range(4)]
        rstdb = [Buf(f"rstd3{i}") for i in range(4)]
        tmp = (junk, junkb, ss, ssb, rstd, rstdb, xn, [Buf("xn30"), Buf("xn31")])

        ostage = [kb.sbuf(st, f"ostage{i}", [128, D], F32) for i in range(2)]
        ostb = [Buf("ost0"), Buf("ost1")]
        osem = [DmaSem(kb, "ost0"), DmaSem(kb, "ost1")]

        def load_x3(t):
            for ts in range(4):
                kb.dma(kb.act, xsem[ts], xh[:, ts, :], h2_scr[t * T + ts * 128:t * T + (ts + 1) * 128, :], writes=[xhb[ts]])

        pf = [kb.sbuf(st, f"pf{i}", [128, D], F32) for i in range(2)]
        pfb = [Buf("pf0"), Buf("pf1")]
        pfsem = [DmaSem(kb, "pf0"), DmaSem(kb, "pf1")]
        ssP = kb.sbuf(st, "ssP", [128, 4], F32)
        rstdP = kb.sbuf(st, "rstdP", [128, 4], F32)
        ssPb = [Buf(f"ssP{i}") for i in range(4)]
        rstdPb = [Buf(f"rstdP{i}") for i in range(4)]
        xn3 = tmp[6]
        xn3b = tmp[7]

        def pf_load(ts, tn):
            b = ts % 2
            kb.dma(kb.act, pfsem[b], pf[b][:], h2_scr[tn * T + ts * 128:tn * T + (ts + 1) * 128, :], writes=[pfb[b]])

        def pf_a(ts, tn):
            b = ts % 2
            kb.op(kb.act, lambda e: e.activation(out=junk[:], in_=pf[b][:], func=AF.Square, accum_out=ssP[:, ts:ts + 1]),
                  reads=[pfb[b]], writes=[junkb, ssPb[ts]])
            kb.op(kb.act, lambda e: e.activation(out=ssP[:, ts:ts + 1], in_=ssP[:, ts:ts + 1], func=AF.Sqrt,
                                                 scale=1.0 / D, bias=mhalf[:, 1:2]),
                  reads=[ssPb[ts], m_buf], writes=[ssPb[ts]])
            kb.op(kb.dve, lambda e: e.reciprocal(out=rstdP[:, ts:ts + 1], in_=ssP[:, ts:ts + 1]),
                  reads=[ssPb[ts]], writes=[rstdPb[ts]])
            kb.op(kb.dve, lambda e: e.scalar_tensor_tensor(out=xn3[b][:], in0=pf[b][:], scalar=rstdP[:, ts:ts + 1],
                                                           in1=g1[:], op0=ALU.mult, op1=ALU.mult),
                  reads=[pfb[b], rstdPb[ts], g1b], writes=[xn3b[b]])
            if ts + 2 < 4:
                pf_load(ts + 2, tn)

        def pf_b(ts):
            b = ts % 2
            for half in range(2):
                bk = 2 * b + half
                pv = banks[bk].bitcast(BF16)
                mms = []
                for k in range(8):
                    dc = half * 8 + k
                    mms.append((pv[:, k * 128:(k + 1) * 128], xn3[b][:, dc * 128:(dc + 1) * 128], ident[:, 0, :]))
                kb.mm_group(mms, reads=[xn3b[b], c_buf], writes=[bbuf[bk]], transpose=True)
                src = pv[:, :].rearrange("p (k t) -> p k t", t=128)
                dst = uT[:, half * 8:half * 8 + 8, ts * 128:(ts + 1) * 128]
                ub = uTb[ts * 2 + half]
                if half == 0:
                    kb.op(kb.act, lambda e: e.activation(out=dst, in_=src, func=AF.Copy), reads=[bbuf[bk]], writes=[ub])
                else:
                    kb.op(kb.dve, lambda e: e.tensor_copy(dst, src), reads=[bbuf[bk]], writes=[ub])

        load_x3(0)
        for t in range(nt):
            rbase = t * T
            if t == 0:
                rmsnorm_to_uT(xh, xhb, g1, g1b, uT, uTb, tmp)
            cb_ = None
            if t + 1 < nt:
                tn_ = t + 1

                def cb_(dp, tn_=tn_):
                    if dp == 0:
                        pf_load(0, tn_)
                        pf_load(1, tn_)
                        pf_a(0, tn_)
                        pf_a(1, tn_)
                    elif dp == 1:
                        pf_b(0)
                        pf_a(2, tn_)
                    elif dp == 2:
                        pf_b(1)
                        pf_a(3, tn_)
                    else:
                        pf_b(2)
                        pf_b(3)
            ffn(stream, W["gu1"], W["dn1"], uT, uTb, actT, actTb, xh, xhb, sg, sgb, after_dp=cb_)
            for ts in range(4):
                ob = ts % 2
                kb.op(kb.act, lambda e: e.activation(out=junk[:], in_=xh[:, ts, :], func=AF.Square, accum_out=ss[:, ts:ts + 1]),
                      reads=[xhb[ts]], writes=[junkb, ssb[ts]])
                kb.op(kb.act, lambda e: e.activation(out=ss[:, ts:ts + 1], in_=ss[:, ts:ts + 1], func=AF.Sqrt,
                                                     scale=1.0 / D, bias=mhalf[:, 1:2]),
                      reads=[ssb[ts], m_buf], writes=[ssb[ts]])
                kb.op(kb.dve, lambda e: e.reciprocal(out=rstd[:, ts:ts + 1], in_=ss[:, ts:ts + 1]),
                      reads=[ssb[ts]], writes=[rstdb[ts]])
                kb.op(kb.dve, lambda e: e.scalar_tensor_tensor(out=ostage[ob][:], in0=xh[:, ts, :], scalar=rstd[:, ts:ts + 1],
                                                               in1=g2[:], op0=ALU.mult, op1=ALU.mult),
                      reads=[xhb[ts], rstdb[ts], g2b], writes=[ostb[ob]])
                kb.dma(kb.act, osem[ob], y_out[rbase + ts * 128:rbase + (ts + 1) * 128, :], ostage[ob][:], reads=[ostb[ob]])
                if t + 1 < nt:
                    kb.dma(kb.act, xsem[ts], xh[:, ts, :], h2_scr[(t + 1) * T + ts * 128:(t + 1) * T + (ts + 1) * 128, :],
                           writes=[xhb[ts]])
        kb.barrier()
    return kb


def host_tables(seqlens, nA):
    bf = ml_dtypes.bfloat16
    ntok = int(sum(seqlens))
    nt = ntok // T
    pos = np.concatenate([np.arange(L) for L in seqlens]).astype(np.int64)
    slen = np.concatenate([np.full(L, L) for L in seqlens]).astype(np.int64)
    sflag = np.zeros(ntok, np.float32)
    start = 0
    for gi, (g0, g1) in enumerate(((0, nA * T), (nA * T, ntok))):
        k = 0
        st = 0
        for L in seqlens:
            if st >= g0 and st < g1:
                sflag[st:st + L] = k
                k += 1
            st += L
    assert sflag.max() <= 1
    a = (pos // 64).astype(np.float32)
    b = (pos % 64).astype(np.float32)
    one = np.ones(ntok, np.float32)
    zero = np.zeros(ntok, np.float32)
    ka1 = np.stack([one, sflag, one, a, one, b, zero, zero])
    ka = np.stack([ka1, -ka1, np.zeros_like(ka1)]).astype(bf)
    qa = np.zeros((NH, 8, ntok), np.float32)
    for h in range(NH):
        m8 = 8.0 * SLOPES[h]
        qa[h] = np.stack([-m8 * BIGPOS * sflag, m8 * BIGPOS * one, -m8 * 64 * a, m8 * 64 * one, -m8 * b, m8 * one, zero, zero])
    qa = qa.astype(bf)
    kk = np.arange(128)[:, None]
    jj = np.arange(896)[None, :]
    dbias = np.abs(jj - 384 - kk).astype(np.float32)
    sid = np.zeros((128, NH + 1, 128), np.float32)
    sid[:, 0] = np.eye(128)
    for h in range(NH):
        sid[:, 1 + h] = -8.0 * SLOPES[h] * np.eye(128)
    sid = sid.astype(bf)
    icnt = np.zeros((4, ntok), np.float32)
    for g, w in enumerate((2, 4, 8, 16)):
        lo = np.clip(pos - w // 2, 0, slen - 1)
        hi = np.clip(pos + w // 2 - 1, 0, slen - 1)
        icnt[g] = 1.0 / (hi - lo + 1).astype(np.float32)
    hmask = np.zeros((128, 2 * nt), np.float32)
    for t in range(nt):
        hmask[:, 2 * t] = 0.0 if pos[t * T] == 0 else 1.0
        hmask[:, 2 * t + 1] = 0.0 if pos[t * T + T - 1] == slen[t * T + T - 1] - 1 else 1.0
    return {"ka_tab": ka, "qa_tab": qa, "dbias_tab": dbias, "sident_tab": sid, "icnt_tab": icnt, "hmask_tab": hmask}


def weight_inputs(inp):
    im = {}
    for i in (1, 2):
        im[f"ffn{i}_w_gu"] = np.ascontiguousarray(inp[f"ffn{i}_w_gu"][0])
        im[f"ffn{i}_w_down"] = np.ascontiguousarray(inp[f"ffn{i}_w_down"][0])
    for n in ("w_in", "w_attn_proj", "w_pool_proj", "w_pool_grp", "w_out"):
        im[n] = np.ascontiguousarray(inp[n][0])
    for n in ("ffn1_norm", "mix_norm", "ffn2_norm"):
        im[n] = np.ascontiguousarray(inp[n]).reshape(1, D)
    im["final_norm"] = np.ascontiguousarray(inp["final_norm"]).reshape(1, D)
    im["lam_vecs"] = np.concatenate([inp["lambda_q1"][0], inp["lambda_k1"][0], inp["lambda_q2"][0],
                                     inp["lambda_k2"][0]]).reshape(1, 256).astype(np.float32)
    im["attn_subln_g"] = np.ascontiguousarray(inp["attn_subln_g"][0]).reshape(128, 1)
    im["pool_scale"] = np.ascontiguousarray(inp["pool_scale"][0].reshape(8, 128).T)
    return im


_PROGRAM = {}


def kernel(**inputs):
    inp = {k: np.asarray(v) for k, v in inputs.items()}
    xp = inp["x_prompt"]
    xs = inp["x_sample"]
    if "full" not in _PROGRAM:
        _PROGRAM["full"] = build_program(nt=12, nA=8)
    kb = _PROGRAM["full"]
    wim = weight_inputs(inp)
    tabs_a = host_tables([4096, 2048], 8)
    tabs_b = host_tables([2048, 2048, 2048], 8)
    in_maps = []
    for c in range(8):
        if c < 4:
            x = np.concatenate([xs[c], xp[c]], axis=0)
            tabs = tabs_a
        else:
            p0 = 4 + 3 * (c - 4)
            x = np.concatenate([xp[p0], xp[p0 + 1], xp[p0 + 2]], axis=0)
            tabs = tabs_b
        im = dict(wim)
        im.update(tabs)
        im["x"] = np.ascontiguousarray(x, dtype=np.float32)
        in_maps.append(im)
    res = run_bass_kernel_spmd(kb.nc, in_maps, core_ids=list(range(8)))
    y_prompt = np.zeros((16, 2048, D), np.float32)
    y_sample = np.zeros((4, 4096, D), np.float32)
    for c in range(8):
        y = np.asarray(res.results[c]["y"])
        if c < 4:
            y_sample[c] = y[0:4096]
            y_prompt[c] = y[4096:6144]
        else:
            p0 = 4 + 3 * (c - 4)
            for i in range(3):
                y_prompt[p0 + i] = y[2048 * i:2048 * (i + 1)]
    return (y_prompt, y_sample)
```
